# Optimizing a Trainium2 kernel written in Bass

```python
import jax, jax.numpy as jnp
from jax import lax
import numpy as np

D_MODEL = 1024
BATCH = 4
SEQ = 8192
DEPTH = 1

RET_HEADS = 4
RET_QK_DIM = 128
RET_V_DIM = 256
RET_CHUNK = 128
ATTN_Q_HEADS = 16
ATTN_KV_HEADS = 2
ATTN_HEAD_DIM = 64
WINDOW = 128
ATTN_BLOCK = 128
D_FF = -(-8 * D_MODEL // (3 * 256)) * 256
ROPE_THETA = 10000.0
EPS = 1e-6

RET_QK = RET_HEADS * RET_QK_DIM
RET_V = RET_HEADS * RET_V_DIM
ATTN_Q = ATTN_Q_HEADS * ATTN_HEAD_DIM
ATTN_KV = ATTN_KV_HEADS * ATTN_HEAD_DIM
SPLITS = [RET_QK, RET_QK, RET_V, RET_V, ATTN_Q, ATTN_KV, ATTN_KV, D_MODEL, D_MODEL]
D_IN = sum(SPLITS)
SPLIT_IDX = [int(v) for v in np.cumsum(SPLITS)[:-1]]

kernel_name = "hybrid_retention_swa_sink_gated_block"


def rms_norm(x, g):
    xf = x.astype(jnp.float32)
    y = xf * lax.rsqrt(jnp.mean(xf * xf, axis=-1, keepdims=True) + EPS)
    return (y * g.astype(jnp.float32)).astype(x.dtype)


def rotary(x, pos):
    d = x.shape[-1]
    half = d // 2
    inv_freq = ROPE_THETA ** (-jnp.arange(half, dtype=jnp.float32) / half)
    ang = pos.astype(jnp.float32)[:, None] * inv_freq[None, :]
    cos = jnp.cos(ang)[None, :, None, :]
    sin = jnp.sin(ang)[None, :, None, :]
    xf = x.astype(jnp.float32)
    x1, x2 = xf[..., :half], xf[..., half:]
    out = jnp.concatenate([x1 * cos - x2 * sin, x2 * cos + x1 * sin], axis=-1)
    return out.astype(x.dtype)


def retention_chunkwise(q, k, v):
    B, S, H, dk = q.shape
    dv = v.shape[-1]
    C = RET_CHUNK
    N = S // C
    log_gamma = jnp.log1p(-jnp.exp2(-5.0 - jnp.arange(H, dtype=jnp.float32)))
    idx = jnp.arange(C, dtype=jnp.float32)
    rel = idx[:, None] - idx[None, :]
    intra_decay = jnp.where(rel[None] >= 0,
                            jnp.exp(log_gamma[:, None, None] * jnp.maximum(rel, 0.0)[None]), 0.0)
    q_decay = jnp.exp(log_gamma[:, None] * (idx + 1.0))[None, :, :, None]
    k_decay = jnp.exp(log_gamma[:, None] * (C - 1.0 - idx))[None, :, :, None]
    chunk_decay = jnp.exp(log_gamma * C)[None, :, None, None]

    qf = q.astype(jnp.float32) * (dk ** -0.5)
    kf = k.astype(jnp.float32)
    vf = v.astype(jnp.float32)
    to_chunks = lambda t: t.reshape(B, N, C, H, t.shape[-1]).transpose(1, 0, 3, 2, 4)
    qc, kc, vc = to_chunks(qf), to_chunks(kf), to_chunks(vf)

    def step(state, inp):
        qn, kn, vn = inp
        scores = jnp.einsum('bhcd,bhsd->bhcs', qn, kn) * intra_decay
        inner = jnp.einsum('bhcs,bhse->bhce', scores, vn)
        cross = jnp.einsum('bhcd,bhde->bhce', qn, state) * q_decay
        new_state = state * chunk_decay + jnp.einsum('bhsd,bhse->bhde', kn * k_decay, vn)
        return new_state, inner + cross

    state0 = jnp.zeros((B, H, dk, dv), jnp.float32)
    _, out = lax.scan(step, state0, (qc, kc, vc))
    return out.transpose(1, 0, 3, 2, 4).reshape(B, S, H, dv)


def head_group_norm(y, g):
    B, S, H, dv = y.shape
    mu = jnp.mean(y, axis=-1, keepdims=True)
    yc = y - mu
    var = jnp.mean(yc * yc, axis=-1, keepdims=True)
    yn = (yc * lax.rsqrt(var + EPS)).reshape(B, S, H * dv)
    return yn * g.astype(jnp.float32)


def sliding_window_sink_attention(q, k, v, sinks):
    B, S, Hq, d = q.shape
    Hkv = k.shape[2]
    G = Hq // Hkv
    C = ATTN_BLOCK
    N = S // C
    qb = q.reshape(B, N, C, Hkv, G, d)
    pad = ((0, 0), (C, 0), (0, 0), (0, 0))
    kp = jnp.pad(k, pad).reshape(B, N + 1, C, Hkv, d)
    vp = jnp.pad(v, pad).reshape(B, N + 1, C, Hkv, d)
    kb = jnp.concatenate([kp[:, :-1], kp[:, 1:]], axis=2)
    vb = jnp.concatenate([vp[:, :-1], vp[:, 1:]], axis=2)
    scores = jnp.einsum('bnqhgd,bnkhd->bnhgqk', qb, kb).astype(jnp.float32) * (d ** -0.5)
    qi = jnp.arange(C)[:, None]
    kj = jnp.arange(2 * C)[None, :]
    rel = C + qi - kj
    key_pos = jnp.arange(N)[:, None, None] * C + kj[None] - C
    mask = (rel[None] >= 0) & (rel[None] < WINDOW) & (key_pos >= 0)
    scores = jnp.where(mask[None, :, None, None], scores, -1e30)
    sink = sinks.astype(jnp.float32).reshape(1, 1, Hkv, G, 1, 1)
    m = jnp.maximum(jnp.max(scores, axis=-1, keepdims=True), sink)
    e = jnp.exp(scores - m)
    probs = e / (jnp.sum(e, axis=-1, keepdims=True) + jnp.exp(sink - m))
    out = jnp.einsum('bnhgqk,bnkhd->bnqhgd', probs.astype(v.dtype), vb)
    return out.reshape(B, S, Hq, d)


def setup_inputs(seed: int = 0) -> dict:
    key = jax.random.key(seed)
    ks = jax.random.split(key, 16)
    nrm = lambda k, shape, fan_in: jax.random.normal(k, shape, jnp.float32) * (fan_in ** -0.5)
    gain = lambda k, shape: 1.0 + 0.02 * jax.random.normal(k, shape, jnp.float32)
    return {
        "x": jax.random.normal(ks[0], (BATCH, SEQ, D_MODEL), jnp.float32),
        "ln1_g": gain(ks[1], (DEPTH, D_MODEL)),
        "w_in": nrm(ks[2], (DEPTH, D_MODEL, D_IN), D_MODEL),
        "b_in": 0.02 * jax.random.normal(ks[3], (DEPTH, D_IN), jnp.float32),
        "ret_norm_g": gain(ks[4], (DEPTH, RET_V)),
        "w_ret_out": nrm(ks[5], (DEPTH, RET_V, D_MODEL), RET_V),
        "attn_sinks": 0.5 * jax.random.normal(ks[6], (DEPTH, ATTN_Q_HEADS), jnp.float32),
        "w_attn_out": nrm(ks[7], (DEPTH, ATTN_Q, D_MODEL), ATTN_Q),
        "w_out": nrm(ks[8], (DEPTH, D_MODEL, D_MODEL), D_MODEL),
        "ln2_g": gain(ks[9], (DEPTH, D_MODEL)),
        "w_ffn_gate": nrm(ks[10], (DEPTH, D_MODEL, D_FF), D_MODEL),
        "w_ffn_up": nrm(ks[11], (DEPTH, D_MODEL, D_FF), D_MODEL),
        "w_ffn_down": nrm(ks[12], (DEPTH, D_FF, D_MODEL), D_FF),
        "lnf_g": gain(ks[13], (D_MODEL,)),
    }


def reference(x, ln1_g, w_in, b_in, ret_norm_g, w_ret_out, attn_sinks, w_attn_out, w_out,
              ln2_g, w_ffn_gate, w_ffn_up, w_ffn_down, lnf_g):
    B, S, _ = x.shape
    pos = jnp.arange(S, dtype=jnp.int32)
    for l in range(DEPTH):
        h = rms_norm(x, ln1_g[l])
        proj = h @ w_in[l] + b_in[l]
        rq, rk, rv, rg, aq, ak, av, gate_a, gate_b = jnp.split(proj, SPLIT_IDX, axis=-1)

        rq = rotary(rq.reshape(B, S, RET_HEADS, RET_QK_DIM), pos)
        rk = rotary(rk.reshape(B, S, RET_HEADS, RET_QK_DIM), pos)
        ry = retention_chunkwise(rq, rk, rv.reshape(B, S, RET_HEADS, RET_V_DIM))
        ry = head_group_norm(ry, ret_norm_g[l]).astype(x.dtype)
        branch_a = (jax.nn.silu(rg) * ry) @ w_ret_out[l]

        aq = rotary(aq.reshape(B, S, ATTN_Q_HEADS, ATTN_HEAD_DIM), pos)
        ak = rotary(ak.reshape(B, S, ATTN_KV_HEADS, ATTN_HEAD_DIM), pos)
        ay = sliding_window_sink_attention(aq, ak, av.reshape(B, S, ATTN_KV_HEADS, ATTN_HEAD_DIM),
                                           attn_sinks[l])
        branch_b = ay.reshape(B, S, ATTN_Q) @ w_attn_out[l]

        merged = jax.nn.sigmoid(gate_a) * branch_a + jax.nn.sigmoid(gate_b) * branch_b
        x = x + merged @ w_out[l]

        h2 = rms_norm(x, ln2_g[l])
        x = x + (jax.nn.silu(h2 @ w_ffn_gate[l]) * (h2 @ w_ffn_up[l])) @ w_ffn_down[l]
    return rms_norm(x, lnf_g)
```

```python
import contextlib
import numpy as np
import concourse.bass as bass
import concourse.mybir as mybir
from concourse.bass_utils import run_bass_kernel_spmd

F32 = mybir.dt.float32
BF16 = mybir.dt.bfloat16
AF = mybir.ActivationFunctionType
ALU = mybir.AluOpType
AX = mybir.AxisListType

D = 1024
DFF = 2816
NFF = DFF // 128
EPS = 1e-6
NCF = 1040
NCB = 640


class Sem:
    def __init__(self, h):
        self.h = h
        self.v = 0


class Buf:
    def __init__(self, name="", excl=False):
        self.name = name
        self.w = {}
        self.r = {}
        self.excl = excl


class Tile:
    def __init__(self, t, name=""):
        self.t = t
        self.b = Buf(name)
        self.sem = None

    def __getitem__(self, k):
        return self.t[k]


def _b(x):
    return x.b if isinstance(x, Tile) else x


class Ctx:
    def __init__(self, nc, es):
        self.nc = nc
        self.es = es
        self.engs = {"pe": nc.tensor, "act": nc.scalar, "dve": nc.vector, "pool": nc.gpsimd, "sp": nc.sync}
        self.sems = {e: Sem(es.enter_context(nc.semaphore("s_" + e))) for e in ("pe", "act", "dve", "pool")}
        self.waited = {e: {} for e in self.engs}
        self.allsems = list(self.sems.values())
        self.nsem = 0

    def newsem(self, name="d"):
        self.nsem += 1
        s = Sem(self.es.enter_context(self.nc.semaphore("%s%d" % (name, self.nsem))))
        self.allsems.append(s)
        return s

    def _emit_waits(self, e, need):
        eng = self.engs[e]
        wd = self.waited[e]
        for s, v in need.items():
            if wd.get(s, 0) < v:
                eng.wait_ge(s.h, v)
                wd[s] = v

    def op(self, e, fn, reads=(), writes=(), strict=False):
        own = self.sems[e]
        need = {}
        for b in reads:
            bb = _b(b)
            for s, v in bb.w.items():
                need[s] = max(need.get(s, 0), v)
            if bb.excl:
                for s, v in bb.r.items():
                    if s is not own:
                        need[s] = max(need.get(s, 0), v)
        for b in writes:
            bb = _b(b)
            for s, v in bb.w.items():
                if strict or s is not own:
                    need[s] = max(need.get(s, 0), v)
            for s, v in bb.r.items():
                if strict or s is not own:
                    need[s] = max(need.get(s, 0), v)
        self._emit_waits(e, need)
        ins = fn(self.engs[e])
        own.v += 1
        ins.then_inc(own.h, 1)
        for b in reads:
            _b(b).r[own] = own.v
        for b in writes:
            bb = _b(b)
            bb.w = {own: own.v}
            bb.r = {}

    def act(self, fn, reads=(), writes=(), strict=False):
        self.op("act", fn, reads, writes, strict)

    def dve(self, fn, reads=(), writes=()):
        self.op("dve", fn, reads, writes)

    def pool(self, fn, reads=(), writes=()):
        self.op("pool", fn, reads, writes)

    def pe(self, fn, reads=(), writes=()):
        self.op("pe", fn, reads, writes)

    def dma(self, q, pairs, sem, reads=(), writes=()):
        need = {}
        for b in reads:
            for s, v in _b(b).w.items():
                need[s] = max(need.get(s, 0), v)
        for b in writes:
            bb = _b(b)
            for s, v in bb.w.items():
                need[s] = max(need.get(s, 0), v)
            for s, v in bb.r.items():
                need[s] = max(need.get(s, 0), v)
        if sem.v > 0:
            need[sem] = max(need.get(sem, 0), sem.v)
        self._emit_waits(q, need)
        eng = self.engs[q]
        for (o, i) in pairs:
            eng.dma_start(out=o, in_=i).then_inc(sem.h, 16)
            sem.v += 16
        for b in reads:
            _b(b).r[sem] = sem.v
        for b in writes:
            bb = _b(b)
            bb.w = {sem: sem.v}
            bb.r = {}

    def barrier(self):
        need = {s: s.v for s in self.allsems if s.v > 0}
        for e in self.engs:
            self._emit_waits(e, dict(need))


def _drain(g):
    for _ in g:
        pass


def _interleave(g1, g2):
    a = g1 is not None
    b = g2 is not None
    while a or b:
        if a:
            try:
                next(g1)
            except StopIteration:
                a = False
        if b:
            try:
                next(g2)
            except StopIteration:
                b = False


def build(NCH, STOP=99):
    assert NCH % 4 == 0
    T = NCH * 128
    nc = bass.Bass("TRN2", target_bir_lowering=False)

    def din(name, shape):
        return nc.dram_tensor(name, shape, F32, kind="ExternalInput").ap()

    x_d = din("x", [T, D])
    xp_d = din("xp", [T, D])
    rot_d = din("rot", [2 * NCH, 128, 192])
    cf_d = din("cf", [128, NCF])
    cb_d = din("cb", [128, NCB])
    w_in_d = din("w_in", [D, 6400])
    b_in_d = din("b_in", [1, 6400])
    g1_d = din("ln1_g", [1, D])
    gr_d = din("ret_norm_g", [1, D])
    wro_d = din("w_ret_out", [D, D])
    sinks_d = din("attn_sinks", [1, 16])
    wao_d = din("w_attn_out", [D, D])
    wo_d = din("w_out", [D, D])
    g2_d = din("ln2_g", [1, D])
    wg_d = din("w_ffn_gate", [D, DFF])
    wu_d = din("w_ffn_up", [D, DFF])
    wd_d = din("w_ffn_down", [DFF, D])
    gf_d = din("lnf_g", [1, D])
    out_d = nc.dram_tensor("out", [T, D], F32, kind="ExternalOutput").ap()
    bas_d = nc.dram_tensor("bas", [T, D], F32, kind="Internal").ap()
    x1s_d = nc.dram_tensor("x1s", [T, D], F32, kind="Internal").ap()
    bias_s_d = nc.dram_tensor("bias_s", [2, 6400], BF16, kind="Internal").ap()
    bas_b = [Buf("bas%d" % i) for i in range(NCH)]
    x1s_b = [Buf("x1s%d" % i) for i in range(NCH)]

    with contextlib.ExitStack() as es0:
        K = Ctx(nc, es0)

        def sb(es, name, shape, dt, dsem=False):
            t = Tile(es.enter_context(nc.sbuf_tensor(name, shape, dt)), name)
            if dsem:
                t.sem = K.newsem()
            return t

        def ring(es, name, n, shape, dt, dsem=False):
            return [sb(es, "%s%d" % (name, i), shape, dt, dsem) for i in range(n)]

        PS = es0.enter_context(nc.psum_tensor("PS", [128, 8, 512], F32))
        pb = [Buf("pb%d" % i, excl=True) for i in range(8)]

        def psT(i):
            return PS[:, i, :].bitcast(BF16).rearrange("p (k t) -> p k t", k=8)

        cft = sb(es0, "cft", [128, NCF], F32, True)
        cbt = sb(es0, "cbt", [128, NCB], BF16, True)
        epst = sb(es0, "epst", [128, 1], F32)
        ssq1 = sb(es0, "ssq1", [128, 2 * NCH], F32)
        rstd1 = sb(es0, "rstd1", [128, 2 * NCH], F32)
        sqt = sb(es0, "sqt", [128, 2 * NCH], F32)
        ssq2 = sb(es0, "ssq2", [128, NCH], F32)
        rstd2 = sb(es0, "rstd2", [128, NCH], F32)
        G1 = [None]
        junk_r = [sb(es0, "junk%d" % i, [128, D], F32) for i in range(1)]
        junk_i = [0]

        def nextjunk():
            junk_i[0] += 1
            return junk_r[0]
        ssq1_b = [Buf() for _ in range(2 * NCH)]
        rstd1_b = [Buf() for _ in range(2 * NCH)]
        ssq2_b = [Buf() for _ in range(NCH)]

        K.dma("sp", [(cft[:], cf_d)], cft.sem, writes=[cft])
        K.dma("pool", [(cbt[:], cb_d)], cbt.sem, writes=[cbt])

        def load_g1(es, name):
            g = sb(es, name, [128, D], F32, True)
            K.dma("sp", [(g[:], g1_d.partition_broadcast(128))], g.sem, writes=[g])
            G1[0] = g
        K.dve(lambda e: e.memset(epst[:], EPS), writes=[epst])
        ident = cbt[:, 0:128]
        mcur = cbt[:, 128:256]
        mprev = cbt[:, 256:384]
        mprev0 = cbt[:, 384:512]
        ones2 = cbt[0:2, 512:640]
        DT = cft[:, 0:512]
        QD = cft[:, 512:1024]
        KDc = cft[:, 1024:1028]
        CDc = cft[:, 1028:1032]
        smask = cft[:, 1032:1033]

        def load_w(q, wt, src, r0, c0, ncols, kchunks):
            pairs = [(wt[:, k, :], src[r0 + k * 128:r0 + (k + 1) * 128, c0:c0 + ncols]) for k in range(kchunks)]
            K.dma(q, pairs, wt.sem, writes=[wt])

        bl32 = sb(es0, "bl32", [128, 50], F32, True)
        bhi = sb(es0, "bhi", [128, 50], BF16, True)
        bhi32 = sb(es0, "bhi32", [128, 50], F32)
        blo = sb(es0, "blo", [128, 50], BF16, True)
        bias_sb = Buf("bias_s")
        K.dma("sp", [(bl32[:], b_in_d.rearrange("o (p j) -> (o p) j", j=50))], bl32.sem, writes=[bl32])
        K.dve(lambda e: e.tensor_copy(out=bhi[:], in_=bl32[:]), reads=[bl32], writes=[bhi])
        K.dve(lambda e: e.tensor_copy(out=bhi32[:], in_=bhi[:]), reads=[bhi], writes=[bhi32])
        K.dve(lambda e: e.tensor_tensor(out=blo[:], in0=bl32[:], in1=bhi32[:], op=ALU.subtract), reads=[bl32, bhi32], writes=[blo])
        K.dma("sp", [(bias_s_d[0:1, :].rearrange("o (p j) -> (o p) j", j=50), bhi[:])], bhi.sem, reads=[bhi], writes=[bias_sb])
        K.dma("sp", [(bias_s_d[1:2, :].rearrange("o (p j) -> (o p) j", j=50), blo[:])], blo.sem, reads=[blo, bias_sb], writes=[bias_sb])

        def load_bias2(es, name, c0, ncols):
            b2 = sb(es, name, [2, ncols], BF16, True)
            K.dma("sp", [(b2[:], bias_s_d[:, c0:c0 + ncols])], b2.sem, reads=[bias_sb], writes=[b2])
            return b2

        def front(es_tiles, n_stat, xsrc, xb, hb, hT, tb, need_stats):
            K.dma("sp", [(xb[:], xsrc)], xb.sem, writes=[xb])
            if need_stats:
                junk = nextjunk()
                K.act(lambda e: e.activation(out=junk[:], in_=xb[:], func=AF.Square, scale=1.0 / 32.0,
                                             accum_out=ssq1[:, n_stat:n_stat + 1]),
                      reads=[xb], writes=[junk, ssq1_b[n_stat]], strict=True)
                K.act(lambda e: e.activation(out=sqt[:, n_stat:n_stat + 1], in_=ssq1[:, n_stat:n_stat + 1],
                                             func=AF.Sqrt, bias=epst[:, 0:1], scale=1.0),
                      reads=[ssq1_b[n_stat], epst], writes=[rstd1_b[n_stat]])
                K.dve(lambda e: e.reciprocal(out=rstd1[:, n_stat:n_stat + 1], in_=sqt[:, n_stat:n_stat + 1]),
                      reads=[rstd1_b[n_stat]], writes=[rstd1_b[n_stat]])
            K.dve(lambda e: e.scalar_tensor_tensor(out=hb[:], in0=xb[:], scalar=rstd1[:, n_stat:n_stat + 1],
                                                   in1=G1[0][:], op0=ALU.mult, op1=ALU.mult),
                  reads=[xb, rstd1_b[n_stat], G1[0]], writes=[hb])

            def tr(e):
                for k in range(8):
                    ins = e.transpose(psT(tb)[:, k, :], hb[:, k * 128:(k + 1) * 128], ident)
                return ins
            K.pe(tr, reads=[hb, cbt], writes=[pb[tb]])
            K.act(lambda e: e.activation(out=hT[:], in_=psT(tb), func=AF.Copy), reads=[pb[tb]], writes=[hT])

        def proj_block(hT, wt, b2, c0, ncols, bank):
            def mm(e):
                for k in range(8):
                    e.matmul(PS[:, bank, 0:ncols], lhsT=hT[:, k, :], rhs=wt[:, k, c0:c0 + ncols], start=(k == 0), stop=False)
                return e.matmul(PS[:, bank, 0:ncols], lhsT=ones2, rhs=b2[0:2, c0:c0 + ncols], start=False, stop=True)
            K.pe(mm, reads=[hT, wt, b2, cbt], writes=[pb[bank]])

        def rotary(bank, nh, half, cos, sin, tA, tB, rb):
            w = nh * 2 * half
            pv = PS[:, bank, 0:w].rearrange("p (h t d) -> p h t d", h=nh, t=2)
            tAv = tA[:, 0:w].rearrange("p (h t d) -> p h t d", h=nh, t=2)
            tBv = tB[:, 0:w].rearrange("p (h t d) -> p h t d", h=nh, t=2)
            cosb = cos.unsqueeze(1).unsqueeze(1).to_broadcast([128, nh, 2, half])
            sinb = sin.unsqueeze(1).to_broadcast([128, nh, half])
            K.dve(lambda e: e.tensor_tensor(out=tAv, in0=pv, in1=cosb, op=ALU.mult), reads=[pb[bank], rb], writes=[tA])
            K.dve(lambda e: e.tensor_tensor(out=tBv[:, :, 0, :], in0=pv[:, :, 1, :], in1=sinb, op=ALU.mult),
                  reads=[pb[bank], rb], writes=[tB])
            K.dve(lambda e: e.tensor_tensor(out=tBv[:, :, 1, :], in0=pv[:, :, 0, :], in1=sinb, op=ALU.mult),
                  reads=[pb[bank], rb], writes=[tB])
            return tAv, tBv

        if STOP <= 0:
            K.barrier()
            return nc
        with contextlib.ExitStack() as es1:
            load_g1(es1, "g1bc_a")
            w1 = sb(es1, "w1", [128, 8, 3072], BF16, True)
            wro = sb(es1, "wro", [128, 8, D], BF16, True)
            grbc = sb(es1, "grbc", [128, D], F32, True)
            load_w("pool", w1, w_in_d, 0, 0, 3072, 8)
            load_w("pool", wro, wro_d, 0, 0, D, 8)
            b2a = load_bias2(es1, "b2a", 0, 3072)
            K.dma("sp", [(grbc[:], gr_d.partition_broadcast(128))], grbc.sem, writes=[grbc])
            K.pool(lambda e: e.tensor_scalar(out=grbc[:], in0=grbc[:], scalar1=0.5, scalar2=None, op0=ALU.mult),
                   reads=[grbc], writes=[grbc])
            xb_r = ring(es1, "xb", 3, [128, D], F32, True)
            rot_r = ring(es1, "rotb", 3, [128, 192], F32, True)
            hb_r = ring(es1, "hb", 2, [128, D], BF16)
            hT_r = ring(es1, "hT", 3, [128, 8, 128], BF16)
            tA = sb(es1, "tA", [128, 512], F32)
            tB = sb(es1, "tB", [128, 512], F32)
            qk_r = ring(es1, "qk", 2, [128, 2, 512], BF16)
            kd_r = ring(es1, "kd", 2, [128, 512], BF16)
            v_r = ring(es1, "vr", 2, [128, D], BF16)
            th_t = sb(es1, "th", [128, D], F32)
            gs_r = ring(es1, "gs", 2, [128, D], F32)
            S32 = sb(es1, "S32", [128, D], F32)
            Sbf = sb(es1, "Sbf", [128, D], BF16)
            qT = sb(es1, "qT", [128, 4, 128], BF16)
            qdT = sb(es1, "qdT", [128, 4, 128], BF16)
            kT = sb(es1, "kT", [128, 4, 128], BF16)
            PT = sb(es1, "PT", [128, 512], BF16)
            bnst = sb(es1, "bnst", [128, 4, 6], F32)
            bnmv = sb(es1, "bnmv", [128, 4, 2], F32)
            bnst_b = [Buf() for _ in range(4)]
            bnmv_b = [Buf() for _ in range(4)]
            st_sum = sb(es1, "st_sum", [128, 4], F32)
            st_ssq = sb(es1, "st_ssq", [128, 4], F32)
            st_mean = sb(es1, "st_mean", [128, 4], F32)
            st_var = sb(es1, "st_var", [128, 4], F32)
            st_rs = sb(es1, "st_rs", [128, 4], F32)
            gsp = sb(es1, "gsp", [128, D], F32)
            gated = sb(es1, "gated", [128, D], BF16)
            gatedT = sb(es1, "gatedT", [128, 8, 128], BF16)
            ba_r = ring(es1, "ba", 2, [128, D], F32, True)

            K.dve(lambda e: e.memset(S32[:], 0.0), writes=[S32])

            def s1_ret(i, n_stat, xsrc, rotidx, full):
                xb = xb_r[i % 3]
                rb = rot_r[i % 3]
                hb = hb_r[i % 2]
                hT = hT_r[i % 3]
                qk = qk_r[i % 2]
                kd = kd_r[i % 2]
                vr = v_r[i % 2]
                gs = gs_r[i % 2]
                K.dma("sp", [(rb[:], rot_d[rotidx])], rb.sem, writes=[rb])
                front(es1, n_stat, xsrc, xb, hb, hT, 0, True)
                yield
                cos = rb[:, 0:64]
                sin = rb[:, 64:128]
                blocks = ([0] if full else []) + [1]
                for j, cb_ in enumerate(blocks):
                    bank = 2 + (j % 2)
                    proj_block(hT, w1, b2a, cb_ * 512, 512, bank)
                    tAv, tBv = rotary(bank, 4, 64, cos, sin, tA, tB, rb)
                    ov = qk[:, cb_, :].rearrange("p (h t d) -> p h t d", h=4, t=2)
                    K.pool(lambda e: e.tensor_tensor(out=ov[:, :, 0, :], in0=tAv[:, :, 0, :], in1=tBv[:, :, 0, :], op=ALU.subtract),
                           reads=[tA, tB], writes=[qk])
                    K.pool(lambda e: e.tensor_tensor(out=ov[:, :, 1, :], in0=tAv[:, :, 1, :], in1=tBv[:, :, 1, :], op=ALU.add),
                           reads=[tA, tB], writes=[qk])
                    yield
                kdb = KDc.unsqueeze(2).to_broadcast([128, 4, 128])
                K.pool(lambda e: e.tensor_tensor(out=kd[:].rearrange("p (h d) -> p h d", h=4),
                                                 in0=qk[:, 1, :].rearrange("p (h d) -> p h d", h=4), in1=kdb, op=ALU.mult),
                       reads=[qk, cft], writes=[kd])
                for j in range(2):
                    bank = 2 + (j % 2)
                    proj_block(hT, w1, b2a, 1024 + j * 512, 512, bank)
                    K.act(lambda e: e.activation(out=vr[:, j * 512:(j + 1) * 512], in_=PS[:, bank, :], func=AF.Copy),
                          reads=[pb[bank]], writes=[vr])
                    yield
                if full:
                    for j in range(2):
                        bank = 2 + (j % 2)
                        proj_block(hT, w1, b2a, 2048 + j * 512, 512, bank)
                        K.act(lambda e: e.activation(out=th_t[:, j * 512:(j + 1) * 512], in_=PS[:, bank, :], func=AF.Tanh, scale=0.5),
                              reads=[pb[bank]], writes=[th_t])
                        K.dve(lambda e: e.scalar_tensor_tensor(out=gs[:, j * 512:(j + 1) * 512], in0=th_t[:, j * 512:(j + 1) * 512],
                                                               scalar=1.0, in1=PS[:, bank, :], op0=ALU.add, op1=ALU.mult),
                              reads=[th_t, pb[bank]], writes=[gs])
                        yield
                    K.pool(lambda e: e.tensor_tensor(out=gs[:], in0=gs[:], in1=grbc[:], op=ALU.mult), reads=[gs, grbc], writes=[gs])

            def state_update(i):
                kd = kd_r[i % 2]
                vr = v_r[i % 2]

                def mm(e):
                    for h in range(4):
                        bank = 7 if h < 2 else 4
                        ins = e.matmul(PS[:, bank, (h % 2) * 256:(h % 2 + 1) * 256], lhsT=kd[:, h * 128:(h + 1) * 128],
                                       rhs=vr[:, h * 256:(h + 1) * 256], start=True, stop=True)
                    return ins
                K.pe(mm, reads=[kd, vr], writes=[pb[7], pb[4]])
                for h in range(4):
                    bank = 7 if h < 2 else 4
                    K.dve(lambda e: e.scalar_tensor_tensor(out=S32[:, h * 256:(h + 1) * 256], in0=S32[:, h * 256:(h + 1) * 256],
                                                           scalar=CDc[:, h:h + 1],
                                                           in1=PS[:, bank, (h % 2) * 256:(h % 2 + 1) * 256],
                                                           op0=ALU.mult, op1=ALU.add),
                          reads=[S32, cft, pb[bank]], writes=[S32])

            gens = {}
            for n in range(NCH + 2):
                if n < NCH:
                    g = s1_ret(n, n, xp_d[n * 128:(n + 1) * 128, :], n, False)
                    next(g)
                    gens[n] = g
                m = n - 2
                if m >= 0:
                    _drain(gens.pop(m))
                    state_update(m)
            K.dve(lambda e: e.tensor_scalar(out=S32[:], in0=S32[:], scalar1=smask, scalar2=None, op0=ALU.mult),
                  reads=[S32, cft], writes=[S32])
            K.pool(lambda e: e.tensor_copy(out=Sbf[:], in_=S32[:]), reads=[S32], writes=[Sbf])

            def s2_ret(n):
                i = n
                qk = qk_r[i % 2]
                vr = v_r[i % 2]
                gs = gs_r[i % 2]
                ba = ba_r[i % 2]

                def tr(e):
                    for j in range(8):
                        ins = e.transpose(psT(1)[:, j, :], qk[:, j // 4, (j % 4) * 128:(j % 4 + 1) * 128], ident)
                    return ins
                K.pe(tr, reads=[qk, cbt], writes=[pb[1]])
                K.act(lambda e: e.activation(out=qT[:], in_=psT(1)[:, 0:4, :], func=AF.Copy), reads=[pb[1]], writes=[qT])
                K.dve(lambda e: e.tensor_tensor(out=qdT[:], in0=psT(1)[:, 0:4, :], in1=QD.rearrange("p (h c) -> p h c", h=4), op=ALU.mult),
                      reads=[pb[1], cft], writes=[qdT])
                K.act(lambda e: e.activation(out=kT[:], in_=psT(1)[:, 4:8, :], func=AF.Copy), reads=[pb[1]], writes=[kT])
                yield

                def sc(e):
                    for h in range(4):
                        ins = e.matmul(PS[:, 4, h * 128:(h + 1) * 128], lhsT=kT[:, h, :], rhs=qT[:, h, :], start=True, stop=True)
                    return ins
                K.pe(sc, reads=[kT, qT], writes=[pb[4]])
                K.dve(lambda e: e.tensor_tensor(out=PT[:], in0=PS[:, 4, :], in1=DT, op=ALU.mult), reads=[pb[4], cft], writes=[PT])
                yield

                def om(e):
                    for h in range(4):
                        o = PS[:, 5 + h // 2, (h % 2) * 256:(h % 2 + 1) * 256]
                        e.matmul(o, lhsT=PT[:, h * 128:(h + 1) * 128], rhs=vr[:, h * 256:(h + 1) * 256], start=True, stop=False)
                        ins = e.matmul(o, lhsT=qdT[:, h, :], rhs=Sbf[:, h * 256:(h + 1) * 256], start=False, stop=True)
                    return ins
                K.pe(om, reads=[PT, vr, qdT, Sbf], writes=[pb[5], pb[6]])
                yield
                state_update(n)
                K.act(lambda e: e.activation(out=Sbf[:], in_=S32[:], func=AF.Copy), reads=[S32], writes=[Sbf])
                yield
                O = PS[:, 5:7, :].rearrange("p b (h e) -> p (b h) e", h=2)
                for h in range(4):
                    K.dve(lambda e: e.bn_stats(out=bnst[:, h, :], in_=O[:, h, :]), reads=[pb[5], pb[6]], writes=[bnst_b[h]])
                for h in range(4):
                    K.dve(lambda e: e.bn_aggr(out=bnmv[:, h, :], in_=bnst[:, h, :]), reads=[bnst_b[h]], writes=[bnmv_b[h]])
                K.act(lambda e: e.activation(out=st_rs[:], in_=bnmv[:, :, 1], func=AF.Sqrt, bias=epst[:, 0:1], scale=1.0),
                      reads=bnmv_b + [epst], writes=[st_rs])
                K.dve(lambda e: e.reciprocal(out=st_rs[:], in_=st_rs[:]), reads=[st_rs], writes=[st_rs])
                yield
                K.dve(lambda e: e.tensor_tensor(out=gsp[:].rearrange("p (h e) -> p h e", h=4), in0=gs[:].rearrange("p (h e) -> p h e", h=4),
                                                in1=st_rs[:].unsqueeze(2).to_broadcast([128, 4, 256]), op=ALU.mult),
                      reads=[gs, st_rs], writes=[gsp])
                for h in range(4):
                    K.dve(lambda e: e.scalar_tensor_tensor(out=gated[:, h * 256:(h + 1) * 256], in0=O[:, h, :], scalar=bnmv[:, h, 0:1],
                                                           in1=gsp[:, h * 256:(h + 1) * 256], op0=ALU.subtract, op1=ALU.mult),
                          reads=[pb[5], pb[6], bnmv_b[h], gsp], writes=[gated])
                yield

                def tr2(e):
                    for k in range(8):
                        ins = e.transpose(psT(1)[:, k, :], gated[:, k * 128:(k + 1) * 128], ident)
                    return ins
                K.pe(tr2, reads=[gated, cbt], writes=[pb[1]])
                K.act(lambda e: e.activation(out=gatedT[:], in_=psT(1), func=AF.Copy), reads=[pb[1]], writes=[gatedT])
                yield

                def bam(e):
                    for half in range(2):
                        for k in range(8):
                            ins = e.matmul(PS[:, 5 + half, :], lhsT=gatedT[:, k, :], rhs=wro[:, k, half * 512:(half + 1) * 512],
                                           start=(k == 0), stop=(k == 7))
                    return ins
                K.pe(bam, reads=[gatedT, wro], writes=[pb[5], pb[6]])
                K.act(lambda e: e.activation(out=ba[:], in_=PS[:, 5:7, :].rearrange("p b c -> p (b c)"), func=AF.Copy),
                      reads=[pb[5], pb[6]], writes=[ba])
                K.dma("sp", [(bas_d[n * 128:(n + 1) * 128, :], ba[:])], ba.sem, reads=[ba], writes=[bas_b[n]])

            _drain(s1_ret(0, NCH, x_d[0:128, :], NCH, True))
            for n in range(NCH):
                nx = n + 1
                g1 = s1_ret(nx, NCH + nx, x_d[nx * 128:(nx + 1) * 128, :], NCH + nx, True) if nx < NCH else None
                _interleave(g1, s2_ret(n))
            K.barrier()
        if STOP <= 2:
            return nc

        with contextlib.ExitStack() as es2:
            load_g1(es2, "g1bc_b")
            w2 = sb(es2, "w2", [128, 8, 3328], BF16, True)
            wao = sb(es2, "wao", [128, 8, D], BF16, True)
            wo = sb(es2, "wo", [128, 8, D], BF16, True)
            load_w("pool", w2, w_in_d, 0, 3072, 3328, 8)
            load_w("pool", wao, wao_d, 0, 0, D, 8)
            load_w("pool", wo, wo_d, 0, 0, D, 8)
            b2b = load_bias2(es2, "b2b", 3072, 3328)
            esk = sb(es2, "esk", [128, 16], F32, True)
            K.dma("sp", [(esk[:], sinks_d.partition_broadcast(128))], esk.sem, writes=[esk])
            K.act(lambda e: e.activation(out=esk[:], in_=esk[:], func=AF.Exp), reads=[esk], writes=[esk])
            xb_r = ring(es2, "xb2", 3, [128, D], F32, True)
            rot_r = ring(es2, "rotb2", 2, [128, 192], F32, True)
            hb_r = ring(es2, "hb2", 2, [128, D], BF16)
            hT_r = ring(es2, "hT2", 2, [128, 8, 128], BF16)
            tA = sb(es2, "tA2", [128, 512], F32)
            tB = sb(es2, "tB2", [128, 512], F32)
            qr_r = ring(es2, "qr", 2, [128, D], BF16)
            kdup_r = ring(es2, "kdup", 3, [128, 4, 64], BF16)
            vaug_r = ring(es2, "vaug", 3, [128, 2, 65], BF16)
            tha_r = ring(es2, "tha", 2, [128, D], F32)
            thb_r = ring(es2, "thb", 2, [128, D], F32)
            qT2 = sb(es2, "qT2", [128, 8, 128], BF16)
            kT_r = ring(es2, "kT2", 3, [128, 2, 128], BF16)
            Ecur = ring(es2, "Ecur", 2, [128, D], BF16)
            Eprv = ring(es2, "Eprv", 2, [128, D], BF16)
            den = sb(es2, "den", [128, 16], F32)
            ay = sb(es2, "ay", [128, D], BF16)
            ayT = sb(es2, "ayT", [128, 8, 128], BF16)
            bab_r = ring(es2, "bab", 2, [128, D], F32, True)
            m1 = sb(es2, "m1", [128, D], F32)
            m2 = sb(es2, "m2", [128, D], F32)
            mg = sb(es2, "mg", [128, D], BF16)
            mgT = sb(es2, "mgT", [128, 8, 128], BF16)
            x1_r = ring(es2, "x1b", 2, [128, D], F32, True)
            for v in vaug_r:
                K.dve(lambda e: e.memset(v[:], 1.0), writes=[v])

            def s1_att(i, n_stat, xsrc, rotidx, full):
                xb = xb_r[i % 3]
                rb = rot_r[i % 2]
                hb = hb_r[i % 2]
                hT = hT_r[i % 2]
                qr = qr_r[i % 2]
                kdup = kdup_r[i % 3]
                vaug = vaug_r[i % 3]
                tha = tha_r[i % 2]
                thb = thb_r[i % 2]
                K.dma("sp", [(rb[:], rot_d[rotidx])], rb.sem, writes=[rb])
                front(es2, n_stat, xsrc, xb, hb, hT, 0, False)
                yield
                cos = rb[:, 128:160]
                sin = rb[:, 160:192]
                bi = 0
                if full:
                    for j in range(2):
                        bank = 2 + (bi % 2)
                        bi += 1
                        proj_block(hT, w2, b2b, j * 512, 512, bank)
                        tAv, tBv = rotary(bank, 8, 32, cos, sin, tA, tB, rb)
                        ov = qr[:, j * 512:(j + 1) * 512].rearrange("p (h t d) -> p h t d", h=8, t=2)
                        K.pool(lambda e: e.tensor_tensor(out=ov[:, :, 0, :], in0=tAv[:, :, 0, :], in1=tBv[:, :, 0, :], op=ALU.subtract),
                               reads=[tA, tB], writes=[qr])
                        K.pool(lambda e: e.tensor_tensor(out=ov[:, :, 1, :], in0=tAv[:, :, 1, :], in1=tBv[:, :, 1, :], op=ALU.add),
                               reads=[tA, tB], writes=[qr])
                        yield
                bank = 2 + (bi % 2)
                bi += 1
                proj_block(hT, w2, b2b, 1024, 256, bank)
                tAv, tBv = rotary(bank, 2, 32, cos, sin, tA, tB, rb)
                kv = kdup[:].rearrange("p (g r) (t d) -> p g r t d", r=2, t=2)
                for r in range(2):
                    K.pool(lambda e: e.tensor_tensor(out=kv[:, :, r, 0, :], in0=tAv[:, :, 0, :], in1=tBv[:, :, 0, :], op=ALU.subtract),
                           reads=[tA, tB], writes=[kdup])
                    K.pool(lambda e: e.tensor_tensor(out=kv[:, :, r, 1, :], in0=tAv[:, :, 1, :], in1=tBv[:, :, 1, :], op=ALU.add),
                           reads=[tA, tB], writes=[kdup])
                K.act(lambda e: e.activation(out=vaug[:, :, 0:64], in_=PS[:, bank, 128:256].rearrange("p (g d) -> p g d", g=2), func=AF.Copy),
                      reads=[pb[bank]], writes=[vaug])
                yield
                if full:
                    for (tht, c0) in ((tha, 1280), (thb, 2304)):
                        for j in range(2):
                            bank = 2 + (bi % 2)
                            bi += 1
                            proj_block(hT, w2, b2b, c0 + j * 512, 512, bank)
                            K.act(lambda e: e.activation(out=tht[:, j * 512:(j + 1) * 512], in_=PS[:, bank, :], func=AF.Tanh, scale=0.5),
                                  reads=[pb[bank]], writes=[tht])
                            yield

            def k_transpose(i):
                kdup = kdup_r[i % 3]
                kT = kT_r[i % 3]

                def tr(e):
                    for j in range(2):
                        ins = e.transpose(psT(0)[:, j, :], kdup[:, 2 * j:2 * j + 2, :].rearrange("p a d -> p (a d)"), ident)
                    return ins
                K.pe(tr, reads=[kdup, cbt], writes=[pb[0]])
                K.act(lambda e: e.activation(out=kT[:], in_=psT(0)[:, 0:2, :], func=AF.Copy), reads=[pb[0]], writes=[kT])

            def s2_att(n):
                i = n + 1
                xb = xb_r[i % 3]
                qr = qr_r[i % 2]
                vcur = vaug_r[i % 3]
                vprv = vaug_r[(i - 1) % 3]
                kTc = kT_r[i % 3]
                kTp = kT_r[(i - 1) % 3]
                tha = tha_r[i % 2]
                thb = thb_r[i % 2]
                bab = bab_r[n % 2]
                x1b = x1_r[n % 2]
                K.dma("sp", [(bab[:], bas_d[n * 128:(n + 1) * 128, :])], bab.sem, reads=[bas_b[n]], writes=[bab])

                def tr(e):
                    for j in range(8):
                        ins = e.transpose(psT(1)[:, j, :], qr[:, j * 128:(j + 1) * 128], ident)
                    return ins
                K.pe(tr, reads=[qr, cbt], writes=[pb[1]])
                K.act(lambda e: e.activation(out=qT2[:], in_=psT(1), func=AF.Copy), reads=[pb[1]], writes=[qT2])
                k_transpose(i)
                yield
                mp = mprev0 if n == 0 else mprev
                for g in range(2):
                    for (kTx, E, b0, mk) in ((kTc, Ecur[g], 4, mcur), (kTp, Eprv[g], 6, mp)):
                        def sc(e):
                            for p in range(2):
                                ins = e.matmul(PS[:, b0 + p, :], lhsT=kTx[p * 64:(p + 1) * 64, g, :],
                                               rhs=qT2[p * 64:(p + 1) * 64, 4 * g:4 * g + 4, :], start=True, stop=True)
                            return ins
                        K.pe(sc, reads=[kTx, qT2], writes=[pb[b0], pb[b0 + 1]])
                        K.act(lambda e: e.activation(out=E[:], in_=PS[:, b0:b0 + 2, :].rearrange("p b c -> p (b c)"), func=AF.Exp, scale=0.125),
                              reads=[pb[b0], pb[b0 + 1]], writes=[E])
                        (K.pool if b0 == 4 else K.dve)(
                            lambda e: e.tensor_tensor(out=E[:].rearrange("p (a q) -> p a q", a=8), in0=E[:].rearrange("p (a q) -> p a q", a=8),
                                                      in1=mk.unsqueeze(1).to_broadcast([128, 8, 128]), op=ALU.mult),
                            reads=[E, cbt], writes=[E])
                        yield

                def hslot(h):
                    return PS[:, 4 + h // 7, (h % 7) * 65:(h % 7) * 65 + 65]

                def pv(e):
                    for h in range(16):
                        g, p, b = h // 8, h % 2, (h % 8) // 2
                        c0 = p * 512 + b * 128
                        e.matmul(hslot(h), lhsT=Ecur[g][:, c0:c0 + 128], rhs=vcur[:, g, :], start=True, stop=False)
                        ins = e.matmul(hslot(h), lhsT=Eprv[g][:, c0:c0 + 128], rhs=vprv[:, g, :], start=False, stop=True)
                    return ins
                K.pe(pv, reads=[Ecur[0], Ecur[1], Eprv[0], Eprv[1], vcur, vprv], writes=[pb[4], pb[5], pb[6]])
                for (bk, h0, nh) in ((4, 0, 7), (5, 7, 7), (6, 14, 2)):
                    ov = PS[:, bk, 0:nh * 65].rearrange("p (h c) -> p h c", c=65)
                    K.dve(lambda e: e.tensor_tensor(out=den[:, h0:h0 + nh], in0=ov[:, :, 64], in1=esk[:, h0:h0 + nh], op=ALU.add),
                          reads=[pb[bk], esk], writes=[den])
                    K.dve(lambda e: e.reciprocal(out=den[:, h0:h0 + nh], in_=den[:, h0:h0 + nh]), reads=[den], writes=[den])
                    K.dve(lambda e: e.tensor_tensor(out=ay[:, h0 * 64:(h0 + nh) * 64].rearrange("p (h d) -> p h d", d=64), in0=ov[:, :, 0:64],
                                                    in1=den[:, h0:h0 + nh].unsqueeze(2).to_broadcast([128, nh, 64]), op=ALU.mult),
                          reads=[pb[bk], den], writes=[ay])
                yield

                def tr2(e):
                    for k in range(8):
                        ins = e.transpose(psT(1)[:, k, :], ay[:, k * 128:(k + 1) * 128], ident)
                    return ins
                K.pe(tr2, reads=[ay, cbt], writes=[pb[1]])
                K.act(lambda e: e.activation(out=ayT[:], in_=psT(1), func=AF.Copy), reads=[pb[1]], writes=[ayT])
                yield

                def bbm(e):
                    for half in range(2):
                        for k in range(8):
                            ins = e.matmul(PS[:, 4 + half, :], lhsT=ayT[:, k, :], rhs=wao[:, k, half * 512:(half + 1) * 512],
                                           start=(k == 0), stop=(k == 7))
                    return ins
                K.pe(bbm, reads=[ayT, wao], writes=[pb[4], pb[5]])
                K.dve(lambda e: e.scalar_tensor_tensor(out=m1[:], in0=tha[:], scalar=1.0, in1=bab[:], op0=ALU.add, op1=ALU.mult),
                      reads=[tha, bab], writes=[m1])
                K.dve(lambda e: e.scalar_tensor_tensor(out=m2[:], in0=thb[:], scalar=1.0, in1=PS[:, 4:6, :].rearrange("p b c -> p (b c)"),
                                                       op0=ALU.add, op1=ALU.mult),
                      reads=[thb, pb[4], pb[5]], writes=[m2])
                K.dve(lambda e: e.tensor_tensor(out=mg[:], in0=m1[:], in1=m2[:], op=ALU.add), reads=[m1, m2], writes=[mg])
                yield

                def tr3(e):
                    for k in range(8):
                        ins = e.transpose(psT(1)[:, k, :], mg[:, k * 128:(k + 1) * 128], ident)
                    return ins
                K.pe(tr3, reads=[mg, cbt], writes=[pb[1]])
                K.act(lambda e: e.activation(out=mgT[:], in_=psT(1), func=AF.Copy), reads=[pb[1]], writes=[mgT])
                yield

                def xom(e):
                    for half in range(2):
                        for k in range(8):
                            ins = e.matmul(PS[:, 6 + half, :], lhsT=mgT[:, k, :], rhs=wo[:, k, half * 512:(half + 1) * 512],
                                           start=(k == 0), stop=(k == 7))
                    return ins
                K.pe(xom, reads=[mgT, wo], writes=[pb[6], pb[7]])
                K.dve(lambda e: e.scalar_tensor_tensor(out=x1b[:], in0=PS[:, 6:8, :].rearrange("p b c -> p (b c)"), scalar=0.5, in1=xb[:],
                                                       op0=ALU.mult, op1=ALU.add),
                      reads=[pb[6], pb[7], xb], writes=[x1b])
                junk = nextjunk()
                K.act(lambda e: e.activation(out=junk[:], in_=x1b[:], func=AF.Square, scale=1.0 / 32.0, accum_out=ssq2[:, n:n + 1]),
                      reads=[x1b], writes=[junk, ssq2_b[n]], strict=True)
                K.dma("sp", [(x1s_d[n * 128:(n + 1) * 128, :], x1b[:])], x1b.sem, reads=[x1b], writes=[x1s_b[n]])

            _drain(s1_att(0, NCH - 1, xp_d[(NCH - 1) * 128:NCH * 128, :], NCH - 1, False))
            k_transpose(0)
            _drain(s1_att(1, NCH, x_d[0:128, :], NCH, True))
            for n in range(NCH):
                nx = n + 1
                g1 = s1_att(nx + 1, NCH + nx, x_d[nx * 128:(nx + 1) * 128, :], NCH + nx, True) if nx < NCH else None
                _interleave(g1, s2_att(n))
            K.barrier()
        if STOP <= 3:
            return nc

        with contextlib.ExitStack() as es3:
            wg = sb(es3, "wg", [128, 8, DFF], BF16, True)
            wu = sb(es3, "wu", [128, 8, DFF], BF16, True)
            wd = sb(es3, "wd", [128, NFF, D], BF16, True)
            load_w("pool", wg, wg_d, 0, 0, DFF, 8)
            load_w("pool", wu, wu_d, 0, 0, DFF, 8)
            load_w("pool", wd, wd_d, 0, 0, D, NFF)
            g2bc = sb(es3, "g2bc", [128, D], F32, True)
            gfbc = sb(es3, "gfbc", [128, D], F32, True)
            K.dma("sp", [(g2bc[:], g2_d.partition_broadcast(128))], g2bc.sem, writes=[g2bc])
            K.dma("sp", [(gfbc[:], gf_d.partition_broadcast(128))], gfbc.sem, writes=[gfbc])
            xa_r = ring(es3, "xa", 1, [128, D], F32, True)
            h2_r = ring(es3, "h2b", 1, [128, D], BF16)
            h2T_r = ring(es3, "h2T", 2, [128, 8, 512], BF16)
            h2T_bs = [[Buf() for _ in range(4)] for _ in range(2)]
            sg_r = ring(es3, "sg", 2, [128, 512], F32)
            actT = sb(es3, "actT", [128, NFF, 512], BF16)
            actT_b = [Buf() for _ in range(NFF)]
            xr_r = ring(es3, "xr", 2, [128, D], F32, True)
            ssq3 = sb(es3, "ssq3", [128, 4], F32)
            rs3 = sb(es3, "rs3", [128, 4], F32)
            s3_b = [Buf() for _ in range(4)]
            ssq2_all = Buf()
            K.act(lambda e: e.activation(out=rstd2[:], in_=ssq2[:], func=AF.Sqrt, bias=epst[:, 0:1], scale=1.0),
                  reads=ssq2_b + [epst], writes=[ssq2_all])
            K.dve(lambda e: e.reciprocal(out=rstd2[:], in_=rstd2[:]), reads=[ssq2_all], writes=[ssq2_all])

            def p3_front(t):
                h2T = h2T_r[t % 2]
                h2T_b = h2T_bs[t % 2]
                for j in range(4):
                    n = 4 * t + j
                    xa = xa_r[0]
                    h2b = h2_r[0]
                    K.dma("sp", [(xa[:], x1s_d[n * 128:(n + 1) * 128, :])], xa.sem, reads=[x1s_b[n]], writes=[xa])
                    K.dve(lambda e: e.scalar_tensor_tensor(out=h2b[:], in0=xa[:], scalar=rstd2[:, n:n + 1], in1=g2bc[:],
                                                           op0=ALU.mult, op1=ALU.mult),
                          reads=[xa, ssq2_all, g2bc], writes=[h2b])
                    tb = j % 2

                    def tr(e):
                        for k in range(8):
                            ins = e.transpose(psT(tb)[:, k, :], h2b[:, k * 128:(k + 1) * 128], ident)
                        return ins
                    K.pe(tr, reads=[h2b, cbt], writes=[pb[tb]])
                    K.act(lambda e: e.activation(out=h2T[:, :, j * 128:(j + 1) * 128], in_=psT(tb), func=AF.Copy),
                          reads=[pb[tb]], writes=[h2T_b[j]])

            def p3_gu(t):
                h2T = h2T_r[t % 2]
                h2T_b = h2T_bs[t % 2]
                for f in range(NFF):
                    gb_, ub_ = (2, 3) if f % 2 == 0 else (4, 5)
                    sg = sg_r[f % 2]

                    def gm(e):
                        for k in range(8):
                            ins = e.matmul(PS[:, gb_, :], lhsT=wg[:, k, f * 128:(f + 1) * 128], rhs=h2T[:, k, :], start=(k == 0), stop=(k == 7))
                        return ins
                    K.pe(gm, reads=[wg] + h2T_b, writes=[pb[gb_]])

                    def um(e):
                        for k in range(8):
                            ins = e.matmul(PS[:, ub_, :], lhsT=wu[:, k, f * 128:(f + 1) * 128], rhs=h2T[:, k, :], start=(k == 0), stop=(k == 7))
                        return ins
                    K.pe(um, reads=[wu] + h2T_b, writes=[pb[ub_]])
                    K.act(lambda e: e.activation(out=sg[:], in_=PS[:, gb_, :], func=AF.Silu), reads=[pb[gb_]], writes=[sg])
                    K.dve(lambda e: e.tensor_tensor(out=actT[:, f, :], in0=sg[:], in1=PS[:, ub_, :], op=ALU.mult),
                          reads=[sg, pb[ub_]], writes=[actT_b[f]])

            def p3_down(t):
                for j in range(4):
                    n = 4 * t + j
                    xr = xr_r[n % 2]
                    b0 = 6 if j % 2 == 0 else 4
                    K.dma("sp", [(xr[:], x1s_d[n * 128:(n + 1) * 128, :])], xr.sem, reads=[x1s_b[n]], writes=[xr])

                    def dm(e):
                        for half in range(2):
                            for f in range(NFF):
                                ins = e.matmul(PS[:, b0 + half, :], lhsT=actT[:, f, j * 128:(j + 1) * 128],
                                               rhs=wd[:, f, half * 512:(half + 1) * 512], start=(f == 0), stop=(f == NFF - 1))
                        return ins
                    K.pe(dm, reads=[wd] + actT_b, writes=[pb[b0], pb[b0 + 1]])
                    K.dve(lambda e: e.tensor_tensor(out=xr[:], in0=xr[:], in1=PS[:, b0:b0 + 2, :].rearrange("p b c -> p (b c)"), op=ALU.add),
                          reads=[xr, pb[b0], pb[b0 + 1]], writes=[xr])
                    junk = nextjunk()
                    K.act(lambda e: e.activation(out=junk[:], in_=xr[:], func=AF.Square, scale=1.0 / 32.0, accum_out=ssq3[:, j:j + 1]),
                          reads=[xr], writes=[junk, s3_b[j]], strict=True)
                    K.act(lambda e: e.activation(out=rs3[:, j:j + 1], in_=ssq3[:, j:j + 1], func=AF.Sqrt, bias=epst[:, 0:1], scale=1.0),
                          reads=[s3_b[j], epst], writes=[s3_b[j]])
                    K.dve(lambda e: e.reciprocal(out=rs3[:, j:j + 1], in_=rs3[:, j:j + 1]), reads=[s3_b[j]], writes=[s3_b[j]])
                    K.dve(lambda e: e.scalar_tensor_tensor(out=xr[:], in0=xr[:], scalar=rs3[:, j:j + 1], in1=gfbc[:], op0=ALU.mult, op1=ALU.mult),
                          reads=[xr, s3_b[j], gfbc], writes=[xr])
                    K.dma("sp", [(out_d[n * 128:(n + 1) * 128, :], xr[:])], xr.sem, reads=[xr])

            NT = NCH // 4
            p3_front(0)
            for t in range(NT):
                p3_gu(t)
                if t + 1 < NT:
                    p3_front(t + 1)
                p3_down(t)
            K.barrier()
    return nc


def _consts(first_half):
    H = 4
    C = 128
    lg = np.log1p(-np.exp2(-5.0 - np.arange(H, dtype=np.float64)))
    idx = np.arange(C, dtype=np.float64)
    scale = 128.0 ** -0.5
    cf = np.zeros((128, NCF), np.float64)
    rel = idx[None, :] - idx[:, None]
    for h in range(H):
        cf[:, h * 128:(h + 1) * 128] = np.where(rel >= 0, np.exp(lg[h] * np.maximum(rel, 0.0)), 0.0) * scale
        cf[:, 512 + h * 128:512 + (h + 1) * 128] = (np.exp(lg[h] * (idx + 1.0)) * scale)[None, :]
        cf[:, 1024 + h] = np.exp(lg[h] * (C - 1.0 - idx))
        cf[:, 1028 + h] = np.exp(lg[h] * C)
    cf[:, 1032] = 0.0 if first_half else 1.0
    cb = np.zeros((128, NCB), np.float32)
    cb[:, 0:128] = np.eye(128)
    kj = np.arange(128)[:, None]
    qi = np.arange(128)[None, :]
    cb[:, 128:256] = (kj <= qi)
    cb[:, 256:384] = (kj > qi)
    cb[:, 384:512] = 0.0 if first_half else (kj > qi)
    cb[:, 512:640] = 1.0
    return cf.astype(np.float32), cb


def _rot(pos):
    pos = pos.astype(np.float32)
    out = np.zeros(pos.shape + (192,), np.float32)
    inv_r = (10000.0 ** (-np.arange(64, dtype=np.float32) / 64)).astype(np.float32)
    inv_a = (10000.0 ** (-np.arange(32, dtype=np.float32) / 32)).astype(np.float32)
    ang_r = pos[..., None] * inv_r
    ang_a = pos[..., None] * inv_a
    out[..., 0:64] = np.cos(ang_r)
    out[..., 64:128] = np.sin(ang_r)
    out[..., 128:160] = np.cos(ang_a)
    out[..., 160:192] = np.sin(ang_a)
    return out


_NC_CACHE = {}


def kernel(x, ln1_g, w_in, b_in, ret_norm_g, w_ret_out, attn_sinks, w_attn_out, w_out,
           ln2_g, w_ffn_gate, w_ffn_up, w_ffn_down, lnf_g):
    f = lambda a: np.ascontiguousarray(np.asarray(a, dtype=np.float32))
    x = f(x)
    B, S, _ = x.shape
    T = S // 2
    NCH = T // 128
    if NCH not in _NC_CACHE:
        _NC_CACHE[NCH] = build(NCH)
    nc = _NC_CACHE[NCH]
    shared = {
        "w_in": f(w_in)[0], "b_in": f(b_in)[0][None, :], "ln1_g": f(ln1_g)[0][None, :], "ret_norm_g": f(ret_norm_g)[0][None, :],
        "w_ret_out": f(w_ret_out)[0], "attn_sinks": f(attn_sinks)[0][None, :], "w_attn_out": f(w_attn_out)[0], "w_out": f(w_out)[0],
        "ln2_g": f(ln2_g)[0][None, :], "w_ffn_gate": f(w_ffn_gate)[0], "w_ffn_up": f(w_ffn_up)[0], "w_ffn_down": f(w_ffn_down)[0],
        "lnf_g": f(lnf_g)[None, :],
    }
    chunkpos = np.arange(S, dtype=np.int64).reshape(2 * NCH, 128)
    in_maps = []
    for b in range(B):
        for half in range(2):
            cf, cb = _consts(half == 0)
            if half == 0:
                xp = np.zeros((T, D), np.float32)
                pos = np.concatenate([chunkpos[:NCH], chunkpos[:NCH]], 0)
            else:
                xp = x[b, :T]
                pos = chunkpos
            m = dict(shared)
            m.update({"x": np.ascontiguousarray(x[b, half * T:(half + 1) * T]), "xp": np.ascontiguousarray(xp),
                      "rot": _rot(pos), "cf": cf, "cb": cb})
            in_maps.append(m)
    res = run_bass_kernel_spmd(nc, in_maps, core_ids=list(range(2 * B)))
    out = np.empty((B, S, D), np.float32)
    for b in range(B):
        for half in range(2):
            out[b, half * T:(half + 1) * T] = res.results[2 * b + half]["out"]
    return out
```

```python
import contextlib
import numpy as np
import concourse.bass as bass
import concourse.mybir as mybir
from concourse.bass_utils import run_bass_kernel_spmd

F32 = mybir.dt.float32
BF16 = mybir.dt.bfloat16
AF = mybir.ActivationFunctionType
ALU = mybir.AluOpType
AX = mybir.AxisListType

D = 1024
DFF = 2816
NFF = DFF // 128
EPS = 1e-6
NCF = 1040
NCB = 640


class Sem:
    def __init__(self, h):
        self.h = h
        self.v = 0


class Buf:
    def __init__(self, name="", excl=False):
        self.name = name
        self.w = {}
        self.r = {}
        self.excl = excl


class Tile:
    def __init__(self, t, name=""):
        self.t = t
        self.b = Buf(name)
        self.sem = None

    def __getitem__(self, k):
        return self.t[k]


def _b(x):
    return x.b if isinstance(x, Tile) else x


class Ctx:
    def __init__(self, nc, es):
        self.nc = nc
        self.es = es
        self.engs = {"pe": nc.tensor, "act": nc.scalar, "dve": nc.vector, "pool": nc.gpsimd, "sp": nc.sync}
        self.sems = {e: Sem(es.enter_context(nc.semaphore("s_" + e))) for e in ("pe", "act", "dve", "pool")}
        self.waited = {e: {} for e in self.engs}
        self.allsems = list(self.sems.values())
        self.nsem = 0

    def newsem(self, name="d"):
        self.nsem += 1
        s = Sem(self.es.enter_context(self.nc.semaphore("%s%d" % (name, self.nsem))))
        self.allsems.append(s)
        return s

    def _emit_waits(self, e, need):
        eng = self.engs[e]
        wd = self.waited[e]
        for s, v in need.items():
            if wd.get(s, 0) < v:
                eng.wait_ge(s.h, v)
                wd[s] = v

    def op(self, e, fn, reads=(), writes=(), strict=False):
        own = self.sems[e]
        need = {}
        for b in reads:
            bb = _b(b)
            for s, v in bb.w.items():
                need[s] = max(need.get(s, 0), v)
            if bb.excl:
                for s, v in bb.r.items():
                    if s is not own:
                        need[s] = max(need.get(s, 0), v)
        for b in writes:
            bb = _b(b)
            for s, v in bb.w.items():
                if strict or s is not own:
                    need[s] = max(need.get(s, 0), v)
            for s, v in bb.r.items():
                if strict or s is not own:
                    need[s] = max(need.get(s, 0), v)
        self._emit_waits(e, need)
        ins = fn(self.engs[e])
        own.v += 1
        ins.then_inc(own.h, 1)
        for b in reads:
            _b(b).r[own] = own.v
        for b in writes:
            bb = _b(b)
            bb.w = {own: own.v}
            bb.r = {}

    def act(self, fn, reads=(), writes=(), strict=False):
        self.op("act", fn, reads, writes, strict)

    def dve(self, fn, reads=(), writes=()):
        self.op("dve", fn, reads, writes)

    def pool(self, fn, reads=(), writes=()):
        self.op("pool", fn, reads, writes)

    def pe(self, fn, reads=(), writes=()):
        self.op("pe", fn, reads, writes)

    def dma(self, q, pairs, sem, reads=(), writes=()):
        need = {}
        for b in reads:
            for s, v in _b(b).w.items():
                need[s] = max(need.get(s, 0), v)
        for b in writes:
            bb = _b(b)
            for s, v in bb.w.items():
                need[s] = max(need.get(s, 0), v)
            for s, v in bb.r.items():
                need[s] = max(need.get(s, 0), v)
        if sem.v > 0:
            need[sem] = max(need.get(sem, 0), sem.v)
        self._emit_waits(q, need)
        eng = self.engs[q]
        for (o, i) in pairs:
            eng.dma_start(out=o, in_=i).then_inc(sem.h, 16)
            sem.v += 16
        for b in reads:
            _b(b).r[sem] = sem.v
        for b in writes:
            bb = _b(b)
            bb.w = {sem: sem.v}
            bb.r = {}

    def barrier(self):
        need = {s: s.v for s in self.allsems if s.v > 0}
        for e in self.engs:
            self._emit_waits(e, dict(need))


def _drain(g):
    for _ in g:
        pass


def _interleave(g1, g2):
    a = g1 is not None
    b = g2 is not None
    while a or b:
        if b:
            try:
                next(g2)
            except StopIteration:
                b = False
        if a:
            try:
                next(g1)
            except StopIteration:
                a = False


def build(NCH, STOP=99):
    assert NCH % 4 == 0
    T = NCH * 128
    nc = bass.Bass("TRN2", target_bir_lowering=False)

    def din(name, shape):
        return nc.dram_tensor(name, shape, F32, kind="ExternalInput").ap()

    x_d = din("x", [T, D])
    xp_d = din("xp", [T, D])
    rot_d = din("rot", [2 * NCH, 128, 192])
    cf_d = din("cf", [128, NCF])
    cb_d = din("cb", [128, NCB])
    w_in_d = din("w_in", [D, 6400])
    b_in_d = din("b_in", [1, 6400])
    g1_d = din("ln1_g", [1, D])
    gr_d = din("ret_norm_g", [1, D])
    wro_d = din("w_ret_out", [D, D])
    sinks_d = din("attn_sinks", [1, 16])
    wao_d = din("w_attn_out", [D, D])
    wo_d = din("w_out", [D, D])
    g2_d = din("ln2_g", [1, D])
    wg_d = din("w_ffn_gate", [D, DFF])
    wu_d = din("w_ffn_up", [D, DFF])
    wd_d = din("w_ffn_down", [DFF, D])
    gf_d = din("lnf_g", [1, D])
    out_d = nc.dram_tensor("out", [T, D], F32, kind="ExternalOutput").ap()
    bas_d = nc.dram_tensor("bas", [T, D], F32, kind="Internal").ap()
    x1s_d = nc.dram_tensor("x1s", [T, D], F32, kind="Internal").ap()
    bias_s_d = nc.dram_tensor("bias_s", [2, 6400], BF16, kind="Internal").ap()
    bas_b = [Buf("bas%d" % i) for i in range(NCH)]
    x1s_b = [Buf("x1s%d" % i) for i in range(NCH)]

    with contextlib.ExitStack() as es0:
        K = Ctx(nc, es0)

        def sb(es, name, shape, dt, dsem=False):
            t = Tile(es.enter_context(nc.sbuf_tensor(name, shape, dt)), name)
            if dsem:
                t.sem = K.newsem()
            return t

        def ring(es, name, n, shape, dt, dsem=False):
            return [sb(es, "%s%d" % (name, i), shape, dt, dsem) for i in range(n)]

        PS = es0.enter_context(nc.psum_tensor("PS", [128, 8, 512], F32))
        pb = [Buf("pb%d" % i, excl=True) for i in range(8)]

        def psT(i):
            return PS[:, i, :].bitcast(BF16).rearrange("p (k t) -> p k t", k=8)

        cft = sb(es0, "cft", [128, NCF], F32, True)
        cbt = sb(es0, "cbt", [128, NCB], BF16, True)
        epst = sb(es0, "epst", [128, 1], F32)
        ssq1 = sb(es0, "ssq1", [128, 2 * NCH], F32)
        rstd1 = sb(es0, "rstd1", [128, 2 * NCH], F32)
        sqt = sb(es0, "sqt", [128, 2 * NCH], F32)
        ssq2 = sb(es0, "ssq2", [128, NCH], F32)
        rstd2 = sb(es0, "rstd2", [128, NCH], F32)
        G1 = [None]
        junk_r = [sb(es0, "junk%d" % i, [128, D], F32) for i in range(1)]
        junk_i = [0]

        def nextjunk():
            junk_i[0] += 1
            return junk_r[0]
        ssq1_b = [Buf() for _ in range(2 * NCH)]
        rstd1_b = [Buf() for _ in range(2 * NCH)]
        ssq2_b = [Buf() for _ in range(NCH)]

        K.dma("sp", [(cft[:], cf_d)], cft.sem, writes=[cft])
        K.dma("pool", [(cbt[:], cb_d)], cbt.sem, writes=[cbt])

        def load_g1(es, name):
            g = sb(es, name, [128, D], F32, True)
            K.dma("sp", [(g[:], g1_d.partition_broadcast(128))], g.sem, writes=[g])
            G1[0] = g
        K.dve(lambda e: e.memset(epst[:], EPS), writes=[epst])
        ident = cbt[:, 0:128]
        mcur = cbt[:, 128:256]
        mprev = cbt[:, 256:384]
        mprev0 = cbt[:, 384:512]
        ones2 = cbt[0:2, 512:640]
        DT = cft[:, 0:512]
        QD = cft[:, 512:1024]
        KDc = cft[:, 1024:1028]
        CDc = cft[:, 1028:1032]
        smask = cft[:, 1032:1033]

        def load_w(q, wt, src, r0, c0, ncols, kchunks):
            pairs = [(wt[:, k, :], src[r0 + k * 128:r0 + (k + 1) * 128, c0:c0 + ncols]) for k in range(kchunks)]
            K.dma(q, pairs, wt.sem, writes=[wt])

        bl32 = sb(es0, "bl32", [128, 50], F32, True)
        bhi = sb(es0, "bhi", [128, 50], BF16, True)
        bhi32 = sb(es0, "bhi32", [128, 50], F32)
        blo = sb(es0, "blo", [128, 50], BF16, True)
        bias_sb = Buf("bias_s")
        K.dma("sp", [(bl32[:], b_in_d.rearrange("o (p j) -> (o p) j", j=50))], bl32.sem, writes=[bl32])
        K.dve(lambda e: e.tensor_copy(out=bhi[:], in_=bl32[:]), reads=[bl32], writes=[bhi])
        K.dve(lambda e: e.tensor_copy(out=bhi32[:], in_=bhi[:]), reads=[bhi], writes=[bhi32])
        K.dve(lambda e: e.tensor_tensor(out=blo[:], in0=bl32[:], in1=bhi32[:], op=ALU.subtract), reads=[bl32, bhi32], writes=[blo])
        K.dma("sp", [(bias_s_d[0:1, :].rearrange("o (p j) -> (o p) j", j=50), bhi[:])], bhi.sem, reads=[bhi], writes=[bias_sb])
        K.dma("sp", [(bias_s_d[1:2, :].rearrange("o (p j) -> (o p) j", j=50), blo[:])], blo.sem, reads=[blo, bias_sb], writes=[bias_sb])

        def load_bias2(es, name, c0, ncols):
            b2 = sb(es, name, [2, ncols], BF16, True)
            K.dma("sp", [(b2[:], bias_s_d[:, c0:c0 + ncols])], b2.sem, reads=[bias_sb], writes=[b2])
            return b2

        def front(es_tiles, n_stat, xsrc, xb, hb, hT, tb, need_stats):
            K.dma("sp", [(xb[:], xsrc)], xb.sem, writes=[xb])
            if need_stats:
                junk = nextjunk()
                K.act(lambda e: e.activation(out=junk[:], in_=xb[:], func=AF.Square, scale=1.0 / 32.0,
                                             accum_out=ssq1[:, n_stat:n_stat + 1]),
                      reads=[xb], writes=[junk, ssq1_b[n_stat]], strict=True)
                K.act(lambda e: e.activation(out=sqt[:, n_stat:n_stat + 1], in_=ssq1[:, n_stat:n_stat + 1],
                                             func=AF.Sqrt, bias=epst[:, 0:1], scale=1.0),
                      reads=[ssq1_b[n_stat], epst], writes=[rstd1_b[n_stat]])
                K.dve(lambda e: e.reciprocal(out=rstd1[:, n_stat:n_stat + 1], in_=sqt[:, n_stat:n_stat + 1]),
                      reads=[rstd1_b[n_stat]], writes=[rstd1_b[n_stat]])
            K.dve(lambda e: e.scalar_tensor_tensor(out=hb[:], in0=xb[:], scalar=rstd1[:, n_stat:n_stat + 1],
                                                   in1=G1[0][:], op0=ALU.mult, op1=ALU.mult),
                  reads=[xb, rstd1_b[n_stat], G1[0]], writes=[hb])

            def tr(e):
                for k in range(8):
                    ins = e.transpose(psT(tb)[:, k, :], hb[:, k * 128:(k + 1) * 128], ident)
                return ins
            K.pe(tr, reads=[hb, cbt], writes=[pb[tb]])
            K.act(lambda e: e.activation(out=hT[:], in_=psT(tb), func=AF.Copy), reads=[pb[tb]], writes=[hT])

        def proj_block(hT, wt, b2, c0, ncols, bank):
            def mm(e):
                for k in range(8):
                    e.matmul(PS[:, bank, 0:ncols], lhsT=hT[:, k, :], rhs=wt[:, k, c0:c0 + ncols], start=(k == 0), stop=False)
                return e.matmul(PS[:, bank, 0:ncols], lhsT=ones2, rhs=b2[0:2, c0:c0 + ncols], start=False, stop=True)
            K.pe(mm, reads=[hT, wt, b2, cbt], writes=[pb[bank]])

        def rotary(bank, nh, half, cos, sin, tA, tB, rb):
            w = nh * 2 * half
            pv = PS[:, bank, 0:w].rearrange("p (h t d) -> p h t d", h=nh, t=2)
            tAv = tA[:, 0:w].rearrange("p (h t d) -> p h t d", h=nh, t=2)
            tBv = tB[:, 0:w].rearrange("p (h t d) -> p h t d", h=nh, t=2)
            cosb = cos.unsqueeze(1).unsqueeze(1).to_broadcast([128, nh, 2, half])
            sinb = sin.unsqueeze(1).to_broadcast([128, nh, half])
            K.dve(lambda e: e.tensor_tensor(out=tAv, in0=pv, in1=cosb, op=ALU.mult), reads=[pb[bank], rb], writes=[tA])
            K.dve(lambda e: e.tensor_tensor(out=tBv[:, :, 0, :], in0=pv[:, :, 1, :], in1=sinb, op=ALU.mult),
                  reads=[pb[bank], rb], writes=[tB])
            K.dve(lambda e: e.tensor_tensor(out=tBv[:, :, 1, :], in0=pv[:, :, 0, :], in1=sinb, op=ALU.mult),
                  reads=[pb[bank], rb], writes=[tB])
            return tAv, tBv

        if STOP <= 0:
            K.barrier()
            return nc
        with contextlib.ExitStack() as es1:
            load_g1(es1, "g1bc_a")
            w1 = sb(es1, "w1", [128, 8, 3072], BF16, True)
            wro = sb(es1, "wro", [128, 8, D], BF16, True)
            grbc = sb(es1, "grbc", [128, D], F32, True)
            load_w("pool", w1, w_in_d, 0, 0, 3072, 8)
            load_w("pool", wro, wro_d, 0, 0, D, 8)
            b2a = load_bias2(es1, "b2a", 0, 3072)
            K.dma("sp", [(grbc[:], gr_d.partition_broadcast(128))], grbc.sem, writes=[grbc])
            K.pool(lambda e: e.tensor_scalar(out=grbc[:], in0=grbc[:], scalar1=0.5, scalar2=None, op0=ALU.mult),
                   reads=[grbc], writes=[grbc])
            xb_r = ring(es1, "xb", 3, [128, D], F32, True)
            rot_r = ring(es1, "rotb", 2, [128, 192], F32, True)
            hb_r = ring(es1, "hb", 2, [128, D], BF16)
            hT_r = ring(es1, "hT", 2, [128, 8, 128], BF16)
            tA = sb(es1, "tA", [128, 512], F32)
            tB = sb(es1, "tB", [128, 512], F32)
            qk_r = ring(es1, "qk", 2, [128, 2, 512], BF16)
            kd_r = ring(es1, "kd", 2, [128, 512], BF16)
            v_r = ring(es1, "vr", 2, [128, D], BF16)
            th_t = sb(es1, "th", [128, D], F32)
            gs_r = ring(es1, "gs", 2, [128, D], F32)
            S32 = sb(es1, "S32", [128, D], F32)
            Sbf = sb(es1, "Sbf", [128, D], BF16)
            qT = sb(es1, "qT", [128, 4, 128], BF16)
            qdT = sb(es1, "qdT", [128, 4, 128], BF16)
            kT = sb(es1, "kT", [128, 4, 128], BF16)
            PT = sb(es1, "PT", [128, 512], BF16)
            bnst = sb(es1, "bnst", [128, 4, 6], F32)
            bnmv = sb(es1, "bnmv", [128, 4, 2], F32)
            bnst_b = [Buf() for _ in range(4)]
            bnmv_b = [Buf() for _ in range(4)]
            st_sum = sb(es1, "st_sum", [128, 4], F32)
            st_ssq = sb(es1, "st_ssq", [128, 4], F32)
            st_mean = sb(es1, "st_mean", [128, 4], F32)
            st_var = sb(es1, "st_var", [128, 4], F32)
            st_rs = sb(es1, "st_rs", [128, 4], F32)
            gsp = sb(es1, "gsp", [128, D], F32)
            gated = sb(es1, "gated", [128, D], BF16)
            gatedT = sb(es1, "gatedT", [128, 8, 128], BF16)
            ba_r = ring(es1, "ba", 2, [128, D], F32, True)

            K.dve(lambda e: e.memset(S32[:], 0.0), writes=[S32])

            def s1_ret(i, n_stat, xsrc, rotidx, full):
                xb = xb_r[i % 3]
                rb = rot_r[i % 2]
                hb = hb_r[i % 2]
                hT = hT_r[i % 2]
                qk = qk_r[i % 2]
                kd = kd_r[i % 2]
                vr = v_r[i % 2]
                gs = gs_r[i % 2]
                K.dma("sp", [(rb[:], rot_d[rotidx])], rb.sem, writes=[rb])
                front(es1, n_stat, xsrc, xb, hb, hT, 0, True)
                yield
                cos = rb[:, 0:64]
                sin = rb[:, 64:128]
                blocks = ([0] if full else []) + [1]
                for j, cb_ in enumerate(blocks):
                    bank = 2 + (j % 2)
                    proj_block(hT, w1, b2a, cb_ * 512, 512, bank)
                    tAv, tBv = rotary(bank, 4, 64, cos, sin, tA, tB, rb)
                    ov = qk[:, cb_, :].rearrange("p (h t d) -> p h t d", h=4, t=2)
                    K.pool(lambda e: e.tensor_tensor(out=ov[:, :, 0, :], in0=tAv[:, :, 0, :], in1=tBv[:, :, 0, :], op=ALU.subtract),
                           reads=[tA, tB], writes=[qk])
                    K.pool(lambda e: e.tensor_tensor(out=ov[:, :, 1, :], in0=tAv[:, :, 1, :], in1=tBv[:, :, 1, :], op=ALU.add),
                           reads=[tA, tB], writes=[qk])
                    yield
                kdb = KDc.unsqueeze(2).to_broadcast([128, 4, 128])
                K.pool(lambda e: e.tensor_tensor(out=kd[:].rearrange("p (h d) -> p h d", h=4),
                                                 in0=qk[:, 1, :].rearrange("p (h d) -> p h d", h=4), in1=kdb, op=ALU.mult),
                       reads=[qk, cft], writes=[kd])
                for j in range(2):
                    bank = 2 + (j % 2)
                    proj_block(hT, w1, b2a, 1024 + j * 512, 512, bank)
                    K.act(lambda e: e.activation(out=vr[:, j * 512:(j + 1) * 512], in_=PS[:, bank, :], func=AF.Copy),
                          reads=[pb[bank]], writes=[vr])
                    yield
                if full:
                    for j in range(2):
                        bank = 2 + (j % 2)
                        proj_block(hT, w1, b2a, 2048 + j * 512, 512, bank)
                        K.act(lambda e: e.activation(out=th_t[:, j * 512:(j + 1) * 512], in_=PS[:, bank, :], func=AF.Tanh, scale=0.5),
                              reads=[pb[bank]], writes=[th_t])
                        K.dve(lambda e: e.scalar_tensor_tensor(out=gs[:, j * 512:(j + 1) * 512], in0=th_t[:, j * 512:(j + 1) * 512],
                                                               scalar=1.0, in1=PS[:, bank, :], op0=ALU.add, op1=ALU.mult),
                              reads=[th_t, pb[bank]], writes=[gs])
                        yield
                    K.pool(lambda e: e.tensor_tensor(out=gs[:], in0=gs[:], in1=grbc[:], op=ALU.mult), reads=[gs, grbc], writes=[gs])

            def state_update(i):
                kd = kd_r[i % 2]
                vr = v_r[i % 2]

                def mm(e):
                    for h in range(4):
                        bank = 7 if h < 2 else 4
                        ins = e.matmul(PS[:, bank, (h % 2) * 256:(h % 2 + 1) * 256], lhsT=kd[:, h * 128:(h + 1) * 128],
                                       rhs=vr[:, h * 256:(h + 1) * 256], start=True, stop=True)
                    return ins
                K.pe(mm, reads=[kd, vr], writes=[pb[7], pb[4]])
                for h in range(4):
                    bank = 7 if h < 2 else 4
                    K.dve(lambda e: e.scalar_tensor_tensor(out=S32[:, h * 256:(h + 1) * 256], in0=S32[:, h * 256:(h + 1) * 256],
                                                           scalar=CDc[:, h:h + 1],
                                                           in1=PS[:, bank, (h % 2) * 256:(h % 2 + 1) * 256],
                                                           op0=ALU.mult, op1=ALU.add),
                          reads=[S32, cft, pb[bank]], writes=[S32])

            gprev = None
            for n in range(NCH + 1):
                g = None
                if n < NCH:
                    g = s1_ret(n, n, xp_d[n * 128:(n + 1) * 128, :], n, False)
                    next(g)
                if gprev is not None:
                    _drain(gprev)
                    state_update(n - 1)
                gprev = g
            K.dve(lambda e: e.tensor_scalar(out=S32[:], in0=S32[:], scalar1=smask, scalar2=None, op0=ALU.mult),
                  reads=[S32, cft], writes=[S32])
            K.pool(lambda e: e.tensor_copy(out=Sbf[:], in_=S32[:]), reads=[S32], writes=[Sbf])

            def s2_ret(n):
                i = n
                qk = qk_r[i % 2]
                vr = v_r[i % 2]
                gs = gs_r[i % 2]
                ba = ba_r[i % 2]

                def tr(e):
                    for j in range(8):
                        ins = e.transpose(psT(1)[:, j, :], qk[:, j // 4, (j % 4) * 128:(j % 4 + 1) * 128], ident)
                    return ins
                K.pe(tr, reads=[qk, cbt], writes=[pb[1]])
                K.act(lambda e: e.activation(out=qT[:], in_=psT(1)[:, 0:4, :], func=AF.Copy), reads=[pb[1]], writes=[qT])
                K.dve(lambda e: e.tensor_tensor(out=qdT[:], in0=psT(1)[:, 0:4, :], in1=QD.rearrange("p (h c) -> p h c", h=4), op=ALU.mult),
                      reads=[pb[1], cft], writes=[qdT])
                K.act(lambda e: e.activation(out=kT[:], in_=psT(1)[:, 4:8, :], func=AF.Copy), reads=[pb[1]], writes=[kT])
                yield

                def sc(e):
                    for h in range(4):
                        ins = e.matmul(PS[:, 4, h * 128:(h + 1) * 128], lhsT=kT[:, h, :], rhs=qT[:, h, :], start=True, stop=True)
                    return ins
                K.pe(sc, reads=[kT, qT], writes=[pb[4]])
                K.dve(lambda e: e.tensor_tensor(out=PT[:], in0=PS[:, 4, :], in1=DT, op=ALU.mult), reads=[pb[4], cft], writes=[PT])
                yield

                def om(e):
                    for h in range(4):
                        o = PS[:, 5 + h // 2, (h % 2) * 256:(h % 2 + 1) * 256]
                        e.matmul(o, lhsT=PT[:, h * 128:(h + 1) * 128], rhs=vr[:, h * 256:(h + 1) * 256], start=True, stop=False)
                        ins = e.matmul(o, lhsT=qdT[:, h, :], rhs=Sbf[:, h * 256:(h + 1) * 256], start=False, stop=True)
                    return ins
                K.pe(om, reads=[PT, vr, qdT, Sbf], writes=[pb[5], pb[6]])
                yield
                state_update(n)
                K.act(lambda e: e.activation(out=Sbf[:], in_=S32[:], func=AF.Copy), reads=[S32], writes=[Sbf])
                yield
                O = PS[:, 5:7, :].rearrange("p b (h e) -> p (b h) e", h=2)
                for h in range(4):
                    K.dve(lambda e: e.bn_stats(out=bnst[:, h, :], in_=O[:, h, :]), reads=[pb[5], pb[6]], writes=[bnst_b[h]])
                for h in range(4):
                    K.dve(lambda e: e.bn_aggr(out=bnmv[:, h, :], in_=bnst[:, h, :]), reads=[bnst_b[h]], writes=[bnmv_b[h]])
                K.act(lambda e: e.activation(out=st_rs[:], in_=bnmv[:, :, 1], func=AF.Sqrt, bias=epst[:, 0:1], scale=1.0),
                      reads=bnmv_b + [epst], writes=[st_rs])
                K.dve(lambda e: e.reciprocal(out=st_rs[:], in_=st_rs[:]), reads=[st_rs], writes=[st_rs])
                yield
                K.dve(lambda e: e.tensor_tensor(out=gsp[:].rearrange("p (h e) -> p h e", h=4), in0=gs[:].rearrange("p (h e) -> p h e", h=4),
                                                in1=st_rs[:].unsqueeze(2).to_broadcast([128, 4, 256]), op=ALU.mult),
                      reads=[gs, st_rs], writes=[gsp])
                for h in range(4):
                    K.dve(lambda e: e.scalar_tensor_tensor(out=gated[:, h * 256:(h + 1) * 256], in0=O[:, h, :], scalar=bnmv[:, h, 0:1],
                                                           in1=gsp[:, h * 256:(h + 1) * 256], op0=ALU.subtract, op1=ALU.mult),
                          reads=[pb[5], pb[6], bnmv_b[h], gsp], writes=[gated])
                yield

                def tr2(e):
                    for k in range(8):
                        ins = e.transpose(psT(1)[:, k, :], gated[:, k * 128:(k + 1) * 128], ident)
                    return ins
                K.pe(tr2, reads=[gated, cbt], writes=[pb[1]])
                K.act(lambda e: e.activation(out=gatedT[:], in_=psT(1), func=AF.Copy), reads=[pb[1]], writes=[gatedT])
                yield

                def bam(e):
                    for half in range(2):
                        for k in range(8):
                            ins = e.matmul(PS[:, 5 + half, :], lhsT=gatedT[:, k, :], rhs=wro[:, k, half * 512:(half + 1) * 512],
                                           start=(k == 0), stop=(k == 7))
                    return ins
                K.pe(bam, reads=[gatedT, wro], writes=[pb[5], pb[6]])
                K.act(lambda e: e.activation(out=ba[:], in_=PS[:, 5:7, :].rearrange("p b c -> p (b c)"), func=AF.Copy),
                      reads=[pb[5], pb[6]], writes=[ba])
                K.dma("sp", [(bas_d[n * 128:(n + 1) * 128, :], ba[:])], ba.sem, reads=[ba], writes=[bas_b[n]])

            _drain(s1_ret(0, NCH, x_d[0:128, :], NCH, True))
            for n in range(NCH):
                nx = n + 1
                g1 = s1_ret(nx, NCH + nx, x_d[nx * 128:(nx + 1) * 128, :], NCH + nx, True) if nx < NCH else None
                _interleave(g1, s2_ret(n))
            K.barrier()
        if STOP <= 2:
            return nc

        with contextlib.ExitStack() as es2:
            load_g1(es2, "g1bc_b")
            w2 = sb(es2, "w2", [128, 8, 3328], BF16, True)
            wao = sb(es2, "wao", [128, 8, D], BF16, True)
            wo = sb(es2, "wo", [128, 8, D], BF16, True)
            load_w("pool", w2, w_in_d, 0, 3072, 3328, 8)
            load_w("pool", wao, wao_d, 0, 0, D, 8)
            load_w("pool", wo, wo_d, 0, 0, D, 8)
            b2b = load_bias2(es2, "b2b", 3072, 3328)
            esk = sb(es2, "esk", [128, 16], F32, True)
            K.dma("sp", [(esk[:], sinks_d.partition_broadcast(128))], esk.sem, writes=[esk])
            K.act(lambda e: e.activation(out=esk[:], in_=esk[:], func=AF.Exp), reads=[esk], writes=[esk])
            xb_r = ring(es2, "xb2", 3, [128, D], F32, True)
            rot_r = ring(es2, "rotb2", 2, [128, 192], F32, True)
            hb_r = ring(es2, "hb2", 2, [128, D], BF16)
            hT_r = ring(es2, "hT2", 2, [128, 8, 128], BF16)
            tA = sb(es2, "tA2", [128, 512], F32)
            tB = sb(es2, "tB2", [128, 512], F32)
            qr_r = ring(es2, "qr", 2, [128, D], BF16)
            kdup_r = ring(es2, "kdup", 3, [128, 4, 64], BF16)
            vaug_r = ring(es2, "vaug", 3, [128, 2, 65], BF16)
            tha_r = ring(es2, "tha", 2, [128, D], F32)
            thb_r = ring(es2, "thb", 2, [128, D], F32)
            qT2 = sb(es2, "qT2", [128, 8, 128], BF16)
            kT_r = ring(es2, "kT2", 3, [128, 2, 128], BF16)
            Ecur = ring(es2, "Ecur", 2, [128, D], BF16)
            Eprv = ring(es2, "Eprv", 2, [128, D], BF16)
            den = sb(es2, "den", [128, 16], F32)
            ay = sb(es2, "ay", [128, D], BF16)
            ayT = sb(es2, "ayT", [128, 8, 128], BF16)
            bab_r = ring(es2, "bab", 2, [128, D], F32, True)
            m1 = sb(es2, "m1", [128, D], F32)
            m2 = sb(es2, "m2", [128, D], F32)
            mg = sb(es2, "mg", [128, D], BF16)
            mgT = sb(es2, "mgT", [128, 8, 128], BF16)
            x1_r = ring(es2, "x1b", 2, [128, D], F32, True)
            for v in vaug_r:
                K.dve(lambda e: e.memset(v[:], 1.0), writes=[v])

            def s1_att(i, n_stat, xsrc, rotidx, full):
                xb = xb_r[i % 3]
                rb = rot_r[i % 2]
                hb = hb_r[i % 2]
                hT = hT_r[i % 2]
                qr = qr_r[i % 2]
                kdup = kdup_r[i % 3]
                vaug = vaug_r[i % 3]
                tha = tha_r[i % 2]
                thb = thb_r[i % 2]
                K.dma("sp", [(rb[:], rot_d[rotidx])], rb.sem, writes=[rb])
                front(es2, n_stat, xsrc, xb, hb, hT, 0, False)
                yield
                cos = rb[:, 128:160]
                sin = rb[:, 160:192]
                bi = 0
                if full:
                    for j in range(2):
                        bank = 2 + (bi % 2)
                        bi += 1
                        proj_block(hT, w2, b2b, j * 512, 512, bank)
                        tAv, tBv = rotary(bank, 8, 32, cos, sin, tA, tB, rb)
                        ov = qr[:, j * 512:(j + 1) * 512].rearrange("p (h t d) -> p h t d", h=8, t=2)
                        K.pool(lambda e: e.tensor_tensor(out=ov[:, :, 0, :], in0=tAv[:, :, 0, :], in1=tBv[:, :, 0, :], op=ALU.subtract),
                               reads=[tA, tB], writes=[qr])
                        K.pool(lambda e: e.tensor_tensor(out=ov[:, :, 1, :], in0=tAv[:, :, 1, :], in1=tBv[:, :, 1, :], op=ALU.add),
                               reads=[tA, tB], writes=[qr])
                        yield
                bank = 2 + (bi % 2)
                bi += 1
                proj_block(hT, w2, b2b, 1024, 256, bank)
                tAv, tBv = rotary(bank, 2, 32, cos, sin, tA, tB, rb)
                kv = kdup[:].rearrange("p (g r) (t d) -> p g r t d", r=2, t=2)
                for r in range(2):
                    K.pool(lambda e: e.tensor_tensor(out=kv[:, :, r, 0, :], in0=tAv[:, :, 0, :], in1=tBv[:, :, 0, :], op=ALU.subtract),
                           reads=[tA, tB], writes=[kdup])
                    K.pool(lambda e: e.tensor_tensor(out=kv[:, :, r, 1, :], in0=tAv[:, :, 1, :], in1=tBv[:, :, 1, :], op=ALU.add),
                           reads=[tA, tB], writes=[kdup])
                K.act(lambda e: e.activation(out=vaug[:, :, 0:64], in_=PS[:, bank, 128:256].rearrange("p (g d) -> p g d", g=2), func=AF.Copy),
                      reads=[pb[bank]], writes=[vaug])
                yield
                if full:
                    for (tht, c0) in ((tha, 1280), (thb, 2304)):
                        for j in range(2):
                            bank = 2 + (bi % 2)
                            bi += 1
                            proj_block(hT, w2, b2b, c0 + j * 512, 512, bank)
                            K.act(lambda e: e.activation(out=tht[:, j * 512:(j + 1) * 512], in_=PS[:, bank, :], func=AF.Tanh, scale=0.5),
                                  reads=[pb[bank]], writes=[tht])
                            yield

            def k_transpose(i):
                kdup = kdup_r[i % 3]
                kT = kT_r[i % 3]

                def tr(e):
                    for j in range(2):
                        ins = e.transpose(psT(0)[:, j, :], kdup[:, 2 * j:2 * j + 2, :].rearrange("p a d -> p (a d)"), ident)
                    return ins
                K.pe(tr, reads=[kdup, cbt], writes=[pb[0]])
                K.act(lambda e: e.activation(out=kT[:], in_=psT(0)[:, 0:2, :], func=AF.Copy), reads=[pb[0]], writes=[kT])

            def s2_att(n):
                i = n + 1
                xb = xb_r[i % 3]
                qr = qr_r[i % 2]
                vcur = vaug_r[i % 3]
                vprv = vaug_r[(i - 1) % 3]
                kTc = kT_r[i % 3]
                kTp = kT_r[(i - 1) % 3]
                tha = tha_r[i % 2]
                thb = thb_r[i % 2]
                bab = bab_r[n % 2]
                x1b = x1_r[n % 2]
                K.dma("sp", [(bab[:], bas_d[n * 128:(n + 1) * 128, :])], bab.sem, reads=[bas_b[n]], writes=[bab])

                def tr(e):
                    for j in range(8):
                        ins = e.transpose(psT(1)[:, j, :], qr[:, j * 128:(j + 1) * 128], ident)
                    return ins
                K.pe(tr, reads=[qr, cbt], writes=[pb[1]])
                K.act(lambda e: e.activation(out=qT2[:], in_=psT(1), func=AF.Copy), reads=[pb[1]], writes=[qT2])
                k_transpose(i)
                yield
                mp = mprev0 if n == 0 else mprev
                for g in range(2):
                    for (kTx, E, b0, mk) in ((kTc, Ecur[g], 4, mcur), (kTp, Eprv[g], 6, mp)):
                        def sc(e):
                            for p in range(2):
                                ins = e.matmul(PS[:, b0 + p, :], lhsT=kTx[p * 64:(p + 1) * 64, g, :],
                                               rhs=qT2[p * 64:(p + 1) * 64, 4 * g:4 * g + 4, :], start=True, stop=True)
                            return ins
                        K.pe(sc, reads=[kTx, qT2], writes=[pb[b0], pb[b0 + 1]])
                        K.act(lambda e: e.activation(out=E[:], in_=PS[:, b0:b0 + 2, :].rearrange("p b c -> p (b c)"), func=AF.Exp, scale=0.125),
                              reads=[pb[b0], pb[b0 + 1]], writes=[E])
                        (K.pool if b0 == 4 else K.dve)(
                            lambda e: e.tensor_tensor(out=E[:].rearrange("p (a q) -> p a q", a=8), in0=E[:].rearrange("p (a q) -> p a q", a=8),
                                                      in1=mk.unsqueeze(1).to_broadcast([128, 8, 128]), op=ALU.mult),
                            reads=[E, cbt], writes=[E])
                        yield

                def hslot(h):
                    return PS[:, 4 + h // 7, (h % 7) * 65:(h % 7) * 65 + 65]

                def pv(e):
                    for h in range(16):
                        g, p, b = h // 8, h % 2, (h % 8) // 2
                        c0 = p * 512 + b * 128
                        e.matmul(hslot(h), lhsT=Ecur[g][:, c0:c0 + 128], rhs=vcur[:, g, :], start=True, stop=False)
                        ins = e.matmul(hslot(h), lhsT=Eprv[g][:, c0:c0 + 128], rhs=vprv[:, g, :], start=False, stop=True)
                    return ins
                K.pe(pv, reads=[Ecur[0], Ecur[1], Eprv[0], Eprv[1], vcur, vprv], writes=[pb[4], pb[5], pb[6]])
                for (bk, h0, nh) in ((4, 0, 7), (5, 7, 7), (6, 14, 2)):
                    ov = PS[:, bk, 0:nh * 65].rearrange("p (h c) -> p h c", c=65)
                    K.dve(lambda e: e.tensor_tensor(out=den[:, h0:h0 + nh], in0=ov[:, :, 64], in1=esk[:, h0:h0 + nh], op=ALU.add),
                          reads=[pb[bk], esk], writes=[den])
                    K.dve(lambda e: e.reciprocal(out=den[:, h0:h0 + nh], in_=den[:, h0:h0 + nh]), reads=[den], writes=[den])
                    K.dve(lambda e: e.tensor_tensor(out=ay[:, h0 * 64:(h0 + nh) * 64].rearrange("p (h d) -> p h d", d=64), in0=ov[:, :, 0:64],
                                                    in1=den[:, h0:h0 + nh].unsqueeze(2).to_broadcast([128, nh, 64]), op=ALU.mult),
                          reads=[pb[bk], den], writes=[ay])
                yield

                def tr2(e):
                    for k in range(8):
                        ins = e.transpose(psT(1)[:, k, :], ay[:, k * 128:(k + 1) * 128], ident)
                    return ins
                K.pe(tr2, reads=[ay, cbt], writes=[pb[1]])
                K.act(lambda e: e.activation(out=ayT[:], in_=psT(1), func=AF.Copy), reads=[pb[1]], writes=[ayT])
                yield

                def bbm(e):
                    for half in range(2):
                        for k in range(8):
                            ins = e.matmul(PS[:, 4 + half, :], lhsT=ayT[:, k, :], rhs=wao[:, k, half * 512:(half + 1) * 512],
                                           start=(k == 0), stop=(k == 7))
                    return ins
                K.pe(bbm, reads=[ayT, wao], writes=[pb[4], pb[5]])
                K.dve(lambda e: e.scalar_tensor_tensor(out=m1[:], in0=tha[:], scalar=1.0, in1=bab[:], op0=ALU.add, op1=ALU.mult),
                      reads=[tha, bab], writes=[m1])
                K.dve(lambda e: e.scalar_tensor_tensor(out=m2[:], in0=thb[:], scalar=1.0, in1=PS[:, 4:6, :].rearrange("p b c -> p (b c)"),
                                                       op0=ALU.add, op1=ALU.mult),
                      reads=[thb, pb[4], pb[5]], writes=[m2])
                K.dve(lambda e: e.tensor_tensor(out=mg[:], in0=m1[:], in1=m2[:], op=ALU.add), reads=[m1, m2], writes=[mg])
                yield

                def tr3(e):
                    for k in range(8):
                        ins = e.transpose(psT(1)[:, k, :], mg[:, k * 128:(k + 1) * 128], ident)
                    return ins
                K.pe(tr3, reads=[mg, cbt], writes=[pb[1]])
                K.act(lambda e: e.activation(out=mgT[:], in_=psT(1), func=AF.Copy), reads=[pb[1]], writes=[mgT])
                yield

                def xom(e):
                    for half in range(2):
                        for k in range(8):
                            ins = e.matmul(PS[:, 6 + half, :], lhsT=mgT[:, k, :], rhs=wo[:, k, half * 512:(half + 1) * 512],
                                           start=(k == 0), stop=(k == 7))
                    return ins
                K.pe(xom, reads=[mgT, wo], writes=[pb[6], pb[7]])
                K.dve(lambda e: e.scalar_tensor_tensor(out=x1b[:], in0=PS[:, 6:8, :].rearrange("p b c -> p (b c)"), scalar=0.5, in1=xb[:],
                                                       op0=ALU.mult, op1=ALU.add),
                      reads=[pb[6], pb[7], xb], writes=[x1b])
                junk = nextjunk()
                K.act(lambda e: e.activation(out=junk[:], in_=x1b[:], func=AF.Square, scale=1.0 / 32.0, accum_out=ssq2[:, n:n + 1]),
                      reads=[x1b], writes=[junk, ssq2_b[n]], strict=True)
                K.dma("sp", [(x1s_d[n * 128:(n + 1) * 128, :], x1b[:])], x1b.sem, reads=[x1b], writes=[x1s_b[n]])

            _drain(s1_att(0, NCH - 1, xp_d[(NCH - 1) * 128:NCH * 128, :], NCH - 1, False))
            k_transpose(0)
            _drain(s1_att(1, NCH, x_d[0:128, :], NCH, True))
            for n in range(NCH):
                nx = n + 1
                g1 = s1_att(nx + 1, NCH + nx, x_d[nx * 128:(nx + 1) * 128, :], NCH + nx, True) if nx < NCH else None
                _interleave(g1, s2_att(n))
            K.barrier()
        if STOP <= 3:
            return nc

        with contextlib.ExitStack() as es3:
            wg = sb(es3, "wg", [128, 8, DFF], BF16, True)
            wu = sb(es3, "wu", [128, 8, DFF], BF16, True)
            wd = sb(es3, "wd", [128, NFF, D], BF16, True)
            load_w("pool", wg, wg_d, 0, 0, DFF, 8)
            load_w("pool", wu, wu_d, 0, 0, DFF, 8)
            load_w("pool", wd, wd_d, 0, 0, D, NFF)
            g2bc = sb(es3, "g2bc", [128, D], F32, True)
            gfbc = sb(es3, "gfbc", [128, D], F32, True)
            K.dma("sp", [(g2bc[:], g2_d.partition_broadcast(128))], g2bc.sem, writes=[g2bc])
            K.dma("sp", [(gfbc[:], gf_d.partition_broadcast(128))], gfbc.sem, writes=[gfbc])
            xa_r = ring(es3, "xa", 1, [128, D], F32, True)
            h2_r = ring(es3, "h2b", 1, [128, D], BF16)
            h2T_r = ring(es3, "h2T", 2, [128, 8, 512], BF16)
            h2T_bs = [[Buf() for _ in range(4)] for _ in range(2)]
            sg_r = ring(es3, "sg", 2, [128, 512], F32)
            actT = sb(es3, "actT", [128, NFF, 512], BF16)
            actT_b = [Buf() for _ in range(NFF)]
            xr_r = ring(es3, "xr", 2, [128, D], F32, True)
            ssq3 = sb(es3, "ssq3", [128, 4], F32)
            rs3 = sb(es3, "rs3", [128, 4], F32)
            s3_b = [Buf() for _ in range(4)]
            ssq2_all = Buf()
            K.act(lambda e: e.activation(out=rstd2[:], in_=ssq2[:], func=AF.Sqrt, bias=epst[:, 0:1], scale=1.0),
                  reads=ssq2_b + [epst], writes=[ssq2_all])
            K.dve(lambda e: e.reciprocal(out=rstd2[:], in_=rstd2[:]), reads=[ssq2_all], writes=[ssq2_all])

            def p3_front(t):
                h2T = h2T_r[t % 2]
                h2T_b = h2T_bs[t % 2]
                for j in range(4):
                    n = 4 * t + j
                    xa = xa_r[0]
                    h2b = h2_r[0]
                    K.dma("sp", [(xa[:], x1s_d[n * 128:(n + 1) * 128, :])], xa.sem, reads=[x1s_b[n]], writes=[xa])
                    K.dve(lambda e: e.scalar_tensor_tensor(out=h2b[:], in0=xa[:], scalar=rstd2[:, n:n + 1], in1=g2bc[:],
                                                           op0=ALU.mult, op1=ALU.mult),
                          reads=[xa, ssq2_all, g2bc], writes=[h2b])
                    tb = j % 2

                    def tr(e):
                        for k in range(8):
                            ins = e.transpose(psT(tb)[:, k, :], h2b[:, k * 128:(k + 1) * 128], ident)
                        return ins
                    K.pe(tr, reads=[h2b, cbt], writes=[pb[tb]])
                    K.act(lambda e: e.activation(out=h2T[:, :, j * 128:(j + 1) * 128], in_=psT(tb), func=AF.Copy),
                          reads=[pb[tb]], writes=[h2T_b[j]])

            def p3_gu(t):
                h2T = h2T_r[t % 2]
                h2T_b = h2T_bs[t % 2]
                for f in range(NFF):
                    gb_, ub_ = (2, 3) if f % 2 == 0 else (4, 5)
                    sg = sg_r[f % 2]

                    def gm(e):
                        for k in range(8):
                            ins = e.matmul(PS[:, gb_, :], lhsT=wg[:, k, f * 128:(f + 1) * 128], rhs=h2T[:, k, :], start=(k == 0), stop=(k == 7))
                        return ins
                    K.pe(gm, reads=[wg] + h2T_b, writes=[pb[gb_]])

                    def um(e):
                        for k in range(8):
                            ins = e.matmul(PS[:, ub_, :], lhsT=wu[:, k, f * 128:(f + 1) * 128], rhs=h2T[:, k, :], start=(k == 0), stop=(k == 7))
                        return ins
                    K.pe(um, reads=[wu] + h2T_b, writes=[pb[ub_]])
                    K.act(lambda e: e.activation(out=sg[:], in_=PS[:, gb_, :], func=AF.Silu), reads=[pb[gb_]], writes=[sg])
                    K.dve(lambda e: e.tensor_tensor(out=actT[:, f, :], in0=sg[:], in1=PS[:, ub_, :], op=ALU.mult),
                          reads=[sg, pb[ub_]], writes=[actT_b[f]])

            def p3_down(t):
                for j in range(4):
                    n = 4 * t + j
                    xr = xr_r[n % 2]
                    b0 = 6 if j % 2 == 0 else 4
                    K.dma("sp", [(xr[:], x1s_d[n * 128:(n + 1) * 128, :])], xr.sem, reads=[x1s_b[n]], writes=[xr])

                    def dm(e):
                        for half in range(2):
                            for f in range(NFF):
                                ins = e.matmul(PS[:, b0 + half, :], lhsT=actT[:, f, j * 128:(j + 1) * 128],
                                               rhs=wd[:, f, half * 512:(half + 1) * 512], start=(f == 0), stop=(f == NFF - 1))
                        return ins
                    K.pe(dm, reads=[wd] + actT_b, writes=[pb[b0], pb[b0 + 1]])
                    K.dve(lambda e: e.tensor_tensor(out=xr[:], in0=xr[:], in1=PS[:, b0:b0 + 2, :].rearrange("p b c -> p (b c)"), op=ALU.add),
                          reads=[xr, pb[b0], pb[b0 + 1]], writes=[xr])
                    junk = nextjunk()
                    K.act(lambda e: e.activation(out=junk[:], in_=xr[:], func=AF.Square, scale=1.0 / 32.0, accum_out=ssq3[:, j:j + 1]),
                          reads=[xr], writes=[junk, s3_b[j]], strict=True)
                    K.act(lambda e: e.activation(out=rs3[:, j:j + 1], in_=ssq3[:, j:j + 1], func=AF.Sqrt, bias=epst[:, 0:1], scale=1.0),
                          reads=[s3_b[j], epst], writes=[s3_b[j]])
                    K.dve(lambda e: e.reciprocal(out=rs3[:, j:j + 1], in_=rs3[:, j:j + 1]), reads=[s3_b[j]], writes=[s3_b[j]])
                    K.dve(lambda e: e.scalar_tensor_tensor(out=xr[:], in0=xr[:], scalar=rs3[:, j:j + 1], in1=gfbc[:], op0=ALU.mult, op1=ALU.mult),
                          reads=[xr, s3_b[j], gfbc], writes=[xr])
                    K.dma("sp", [(out_d[n * 128:(n + 1) * 128, :], xr[:])], xr.sem, reads=[xr])

            NT = NCH // 4
            p3_front(0)
            for t in range(NT):
                p3_gu(t)
                if t + 1 < NT:
                    p3_front(t + 1)
                p3_down(t)
            K.barrier()
    return nc


def _consts(first_half):
    H = 4
    C = 128
    lg = np.log1p(-np.exp2(-5.0 - np.arange(H, dtype=np.float64)))
    idx = np.arange(C, dtype=np.float64)
    scale = 128.0 ** -0.5
    cf = np.zeros((128, NCF), np.float64)
    rel = idx[None, :] - idx[:, None]
    for h in range(H):
        cf[:, h * 128:(h + 1) * 128] = np.where(rel >= 0, np.exp(lg[h] * np.maximum(rel, 0.0)), 0.0) * scale
        cf[:, 512 + h * 128:512 + (h + 1) * 128] = (np.exp(lg[h] * (idx + 1.0)) * scale)[None, :]
        cf[:, 1024 + h] = np.exp(lg[h] * (C - 1.0 - idx))
        cf[:, 1028 + h] = np.exp(lg[h] * C)
    cf[:, 1032] = 0.0 if first_half else 1.0
    cb = np.zeros((128, NCB), np.float32)
    cb[:, 0:128] = np.eye(128)
    kj = np.arange(128)[:, None]
    qi = np.arange(128)[None, :]
    cb[:, 128:256] = (kj <= qi)
    cb[:, 256:384] = (kj > qi)
    cb[:, 384:512] = 0.0 if first_half else (kj > qi)
    cb[:, 512:640] = 1.0
    return cf.astype(np.float32), cb


def _rot(pos):
    pos = pos.astype(np.float32)
    out = np.zeros(pos.shape + (192,), np.float32)
    inv_r = (10000.0 ** (-np.arange(64, dtype=np.float32) / 64)).astype(np.float32)
    inv_a = (10000.0 ** (-np.arange(32, dtype=np.float32) / 32)).astype(np.float32)
    ang_r = pos[..., None] * inv_r
    ang_a = pos[..., None] * inv_a
    out[..., 0:64] = np.cos(ang_r)
    out[..., 64:128] = np.sin(ang_r)
    out[..., 128:160] = np.cos(ang_a)
    out[..., 160:192] = np.sin(ang_a)
    return out


_NC_CACHE = {}


def kernel(x, ln1_g, w_in, b_in, ret_norm_g, w_ret_out, attn_sinks, w_attn_out, w_out,
           ln2_g, w_ffn_gate, w_ffn_up, w_ffn_down, lnf_g):
    f = lambda a: np.ascontiguousarray(np.asarray(a, dtype=np.float32))
    x = f(x)
    B, S, _ = x.shape
    T = S // 2
    NCH = T // 128
    if NCH not in _NC_CACHE:
        _NC_CACHE[NCH] = build(NCH)
    nc = _NC_CACHE[NCH]
    shared = {
        "w_in": f(w_in)[0], "b_in": f(b_in)[0][None, :], "ln1_g": f(ln1_g)[0][None, :], "ret_norm_g": f(ret_norm_g)[0][None, :],
        "w_ret_out": f(w_ret_out)[0], "attn_sinks": f(attn_sinks)[0][None, :], "w_attn_out": f(w_attn_out)[0], "w_out": f(w_out)[0],
        "ln2_g": f(ln2_g)[0][None, :], "w_ffn_gate": f(w_ffn_gate)[0], "w_ffn_up": f(w_ffn_up)[0], "w_ffn_down": f(w_ffn_down)[0],
        "lnf_g": f(lnf_g)[None, :],
    }
    chunkpos = np.arange(S, dtype=np.int64).reshape(2 * NCH, 128)
    in_maps = []
    for b in range(B):
        for half in range(2):
            cf, cb = _consts(half == 0)
            if half == 0:
                xp = np.zeros((T, D), np.float32)
                pos = np.concatenate([chunkpos[:NCH], chunkpos[:NCH]], 0)
            else:
                xp = x[b, :T]
                pos = chunkpos
            m = dict(shared)
            m.update({"x": np.ascontiguousarray(x[b, half * T:(half + 1) * T]), "xp": np.ascontiguousarray(xp),
                      "rot": _rot(pos), "cf": cf, "cb": cb})
            in_maps.append(m)
    res = run_bass_kernel_spmd(nc, in_maps, core_ids=list(range(2 * B)))
    out = np.empty((B, S, D), np.float32)
    for b in range(B):
        for half in range(2):
            out[b, half * T:(half + 1) * T] = res.results[2 * b + half]["out"]
    return out
```

```python
import contextlib
import numpy as np
import concourse.bass as bass
import concourse.mybir as mybir
from concourse.bass_utils import run_bass_kernel_spmd

F32 = mybir.dt.float32
BF16 = mybir.dt.bfloat16
AF = mybir.ActivationFunctionType
ALU = mybir.AluOpType
AX = mybir.AxisListType

D = 1024
DFF = 2816
NFF = DFF // 128
EPS = 1e-6
NCF = 1040
NCB = 640


class Sem:
    def __init__(self, h):
        self.h = h
        self.v = 0


class Buf:
    def __init__(self, name="", excl=False):
        self.name = name
        self.w = {}
        self.r = {}
        self.excl = excl


class Tile:
    def __init__(self, t, name=""):
        self.t = t
        self.b = Buf(name)
        self.sem = None

    def __getitem__(self, k):
        return self.t[k]


def _b(x):
    return x.b if isinstance(x, Tile) else x


class Ctx:
    def __init__(self, nc, es):
        self.nc = nc
        self.es = es
        self.engs = {"pe": nc.tensor, "act": nc.scalar, "dve": nc.vector, "pool": nc.gpsimd, "sp": nc.sync}
        self.sems = {e: Sem(es.enter_context(nc.semaphore("s_" + e))) for e in ("pe", "act", "dve", "pool")}
        self.waited = {e: {} for e in self.engs}
        self.allsems = list(self.sems.values())
        self.nsem = 0

    def newsem(self, name="d"):
        self.nsem += 1
        s = Sem(self.es.enter_context(self.nc.semaphore("%s%d" % (name, self.nsem))))
        self.allsems.append(s)
        return s

    def _emit_waits(self, e, need):
        eng = self.engs[e]
        wd = self.waited[e]
        for s, v in need.items():
            if wd.get(s, 0) < v:
                eng.wait_ge(s.h, v)
                wd[s] = v

    def op(self, e, fn, reads=(), writes=(), strict=False):
        own = self.sems[e]
        need = {}
        for b in reads:
            bb = _b(b)
            for s, v in bb.w.items():
                need[s] = max(need.get(s, 0), v)
            if bb.excl:
                for s, v in bb.r.items():
                    if s is not own:
                        need[s] = max(need.get(s, 0), v)
        for b in writes:
            bb = _b(b)
            for s, v in bb.w.items():
                if strict or s is not own:
                    need[s] = max(need.get(s, 0), v)
            for s, v in bb.r.items():
                if strict or s is not own:
                    need[s] = max(need.get(s, 0), v)
        self._emit_waits(e, need)
        ins = fn(self.engs[e])
        own.v += 1
        ins.then_inc(own.h, 1)
        for b in reads:
            _b(b).r[own] = own.v
        for b in writes:
            bb = _b(b)
            bb.w = {own: own.v}
            bb.r = {}

    def act(self, fn, reads=(), writes=(), strict=False):
        self.op("act", fn, reads, writes, strict)

    def dve(self, fn, reads=(), writes=()):
        self.op("dve", fn, reads, writes)

    def pool(self, fn, reads=(), writes=()):
        self.op("pool", fn, reads, writes)

    def pe(self, fn, reads=(), writes=()):
        self.op("pe", fn, reads, writes)

    def dma(self, q, pairs, sem, reads=(), writes=()):
        need = {}
        for b in reads:
            for s, v in _b(b).w.items():
                need[s] = max(need.get(s, 0), v)
        for b in writes:
            bb = _b(b)
            for s, v in bb.w.items():
                need[s] = max(need.get(s, 0), v)
            for s, v in bb.r.items():
                need[s] = max(need.get(s, 0), v)
        if sem.v > 0:
            need[sem] = max(need.get(sem, 0), sem.v)
        self._emit_waits(q, need)
        eng = self.engs[q]
        for (o, i) in pairs:
            eng.dma_start(out=o, in_=i).then_inc(sem.h, 16)
            sem.v += 16
        for b in reads:
            _b(b).r[sem] = sem.v
        for b in writes:
            bb = _b(b)
            bb.w = {sem: sem.v}
            bb.r = {}

    def barrier(self):
        need = {s: s.v for s in self.allsems if s.v > 0}
        for e in self.engs:
            self._emit_waits(e, dict(need))


def _drain(g):
    for _ in g:
        pass


def _interleave(g1, g2):
    a = g1 is not None
    b = g2 is not None
    while a or b:
        if b:
            try:
                next(g2)
                next(g2)
            except StopIteration:
                b = False
        if a:
            try:
                next(g1)
            except StopIteration:
                a = False


def build(NCH, STOP=99):
    assert NCH % 4 == 0
    T = NCH * 128
    nc = bass.Bass("TRN2", target_bir_lowering=False)

    def din(name, shape):
        return nc.dram_tensor(name, shape, F32, kind="ExternalInput").ap()

    x_d = din("x", [T, D])
    xp_d = din("xp", [T, D])
    rot_d = din("rot", [2 * NCH, 128, 192])
    cf_d = din("cf", [128, NCF])
    cb_d = din("cb", [128, NCB])
    w_in_d = din("w_in", [D, 6400])
    b_in_d = din("b_in", [1, 6400])
    g1_d = din("ln1_g", [1, D])
    gr_d = din("ret_norm_g", [1, D])
    wro_d = din("w_ret_out", [D, D])
    sinks_d = din("attn_sinks", [1, 16])
    wao_d = din("w_attn_out", [D, D])
    wo_d = din("w_out", [D, D])
    g2_d = din("ln2_g", [1, D])
    wg_d = din("w_ffn_gate", [D, DFF])
    wu_d = din("w_ffn_up", [D, DFF])
    wd_d = din("w_ffn_down", [DFF, D])
    gf_d = din("lnf_g", [1, D])
    out_d = nc.dram_tensor("out", [T, D], F32, kind="ExternalOutput").ap()
    bas_d = nc.dram_tensor("bas", [T, D], F32, kind="Internal").ap()
    x1s_d = nc.dram_tensor("x1s", [T, D], F32, kind="Internal").ap()
    bias_s_d = nc.dram_tensor("bias_s", [2, 6400], BF16, kind="Internal").ap()
    bas_b = [Buf("bas%d" % i) for i in range(NCH)]
    x1s_b = [Buf("x1s%d" % i) for i in range(NCH)]

    with contextlib.ExitStack() as es0:
        K = Ctx(nc, es0)

        def sb(es, name, shape, dt, dsem=False):
            t = Tile(es.enter_context(nc.sbuf_tensor(name, shape, dt)), name)
            if dsem:
                t.sem = K.newsem()
            return t

        def ring(es, name, n, shape, dt, dsem=False):
            return [sb(es, "%s%d" % (name, i), shape, dt, dsem) for i in range(n)]

        PS = es0.enter_context(nc.psum_tensor("PS", [128, 8, 512], F32))
        pb = [Buf("pb%d" % i, excl=True) for i in range(8)]

        def psT(i):
            return PS[:, i, :].bitcast(BF16).rearrange("p (k t) -> p k t", k=8)

        cft = sb(es0, "cft", [128, NCF], F32, True)
        cbt = sb(es0, "cbt", [128, NCB], BF16, True)
        epst = sb(es0, "epst", [128, 1], F32)
        ssq1 = sb(es0, "ssq1", [128, 2 * NCH], F32)
        rstd1 = sb(es0, "rstd1", [128, 2 * NCH], F32)
        sqt = sb(es0, "sqt", [128, 2 * NCH], F32)
        ssq2 = sb(es0, "ssq2", [128, NCH], F32)
        rstd2 = sb(es0, "rstd2", [128, NCH], F32)
        G1 = [None]
        junk_r = [sb(es0, "junk%d" % i, [128, D], F32) for i in range(1)]
        junk_i = [0]

        def nextjunk():
            junk_i[0] += 1
            return junk_r[0]
        ssq1_b = [Buf() for _ in range(2 * NCH)]
        rstd1_b = [Buf() for _ in range(2 * NCH)]
        ssq2_b = [Buf() for _ in range(NCH)]

        K.dma("sp", [(cft[:], cf_d)], cft.sem, writes=[cft])
        K.dma("pool", [(cbt[:], cb_d)], cbt.sem, writes=[cbt])

        def load_g1(es, name):
            g = sb(es, name, [128, D], F32, True)
            K.dma("sp", [(g[:], g1_d.partition_broadcast(128))], g.sem, writes=[g])
            G1[0] = g
        K.dve(lambda e: e.memset(epst[:], EPS), writes=[epst])
        ident = cbt[:, 0:128]
        mcur = cbt[:, 128:256]
        mprev = cbt[:, 256:384]
        mprev0 = cbt[:, 384:512]
        ones2 = cbt[0:2, 512:640]
        DT = cft[:, 0:512]
        QD = cft[:, 512:1024]
        KDc = cft[:, 1024:1028]
        CDc = cft[:, 1028:1032]
        smask = cft[:, 1032:1033]

        def load_w(q, wt, src, r0, c0, ncols, kchunks):
            pairs = [(wt[:, k, :], src[r0 + k * 128:r0 + (k + 1) * 128, c0:c0 + ncols]) for k in range(kchunks)]
            K.dma(q, pairs, wt.sem, writes=[wt])

        bl32 = sb(es0, "bl32", [128, 50], F32, True)
        bhi = sb(es0, "bhi", [128, 50], BF16, True)
        bhi32 = sb(es0, "bhi32", [128, 50], F32)
        blo = sb(es0, "blo", [128, 50], BF16, True)
        bias_sb = Buf("bias_s")
        K.dma("sp", [(bl32[:], b_in_d.rearrange("o (p j) -> (o p) j", j=50))], bl32.sem, writes=[bl32])
        K.dve(lambda e: e.tensor_copy(out=bhi[:], in_=bl32[:]), reads=[bl32], writes=[bhi])
        K.dve(lambda e: e.tensor_copy(out=bhi32[:], in_=bhi[:]), reads=[bhi], writes=[bhi32])
        K.dve(lambda e: e.tensor_tensor(out=blo[:], in0=bl32[:], in1=bhi32[:], op=ALU.subtract), reads=[bl32, bhi32], writes=[blo])
        K.dma("sp", [(bias_s_d[0:1, :].rearrange("o (p j) -> (o p) j", j=50), bhi[:])], bhi.sem, reads=[bhi], writes=[bias_sb])
        K.dma("sp", [(bias_s_d[1:2, :].rearrange("o (p j) -> (o p) j", j=50), blo[:])], blo.sem, reads=[blo, bias_sb], writes=[bias_sb])

        def load_bias2(es, name, c0, ncols):
            b2 = sb(es, name, [2, ncols], BF16, True)
            K.dma("sp", [(b2[:], bias_s_d[:, c0:c0 + ncols])], b2.sem, reads=[bias_sb], writes=[b2])
            return b2

        def front(es_tiles, n_stat, xsrc, xb, hb, hT, tb, need_stats):
            K.dma("sp", [(xb[:], xsrc)], xb.sem, writes=[xb])
            if need_stats:
                junk = nextjunk()
                K.act(lambda e: e.activation(out=junk[:], in_=xb[:], func=AF.Square, scale=1.0 / 32.0,
                                             accum_out=ssq1[:, n_stat:n_stat + 1]),
                      reads=[xb], writes=[junk, ssq1_b[n_stat]], strict=True)
                K.act(lambda e: e.activation(out=sqt[:, n_stat:n_stat + 1], in_=ssq1[:, n_stat:n_stat + 1],
                                             func=AF.Sqrt, bias=epst[:, 0:1], scale=1.0),
                      reads=[ssq1_b[n_stat], epst], writes=[rstd1_b[n_stat]])
                K.dve(lambda e: e.reciprocal(out=rstd1[:, n_stat:n_stat + 1], in_=sqt[:, n_stat:n_stat + 1]),
                      reads=[rstd1_b[n_stat]], writes=[rstd1_b[n_stat]])
            K.dve(lambda e: e.scalar_tensor_tensor(out=hb[:], in0=xb[:], scalar=rstd1[:, n_stat:n_stat + 1],
                                                   in1=G1[0][:], op0=ALU.mult, op1=ALU.mult),
                  reads=[xb, rstd1_b[n_stat], G1[0]], writes=[hb])

            def tr(e):
                for k in range(8):
                    ins = e.transpose(psT(tb)[:, k, :], hb[:, k * 128:(k + 1) * 128], ident)
                return ins
            K.pe(tr, reads=[hb, cbt], writes=[pb[tb]])
            K.act(lambda e: e.activation(out=hT[:], in_=psT(tb), func=AF.Copy), reads=[pb[tb]], writes=[hT])

        def proj_block(hT, wt, b2, c0, ncols, bank):
            def mm(e):
                for k in range(8):
                    e.matmul(PS[:, bank, 0:ncols], lhsT=hT[:, k, :], rhs=wt[:, k, c0:c0 + ncols], start=(k == 0), stop=False)
                return e.matmul(PS[:, bank, 0:ncols], lhsT=ones2, rhs=b2[0:2, c0:c0 + ncols], start=False, stop=True)
            K.pe(mm, reads=[hT, wt, b2, cbt], writes=[pb[bank]])

        def rotary(bank, nh, half, cos, sin, tA, tB, rb):
            w = nh * 2 * half
            pv = PS[:, bank, 0:w].rearrange("p (h t d) -> p h t d", h=nh, t=2)
            tAv = tA[:, 0:w].rearrange("p (h t d) -> p h t d", h=nh, t=2)
            tBv = tB[:, 0:w].rearrange("p (h t d) -> p h t d", h=nh, t=2)
            cosb = cos.unsqueeze(1).unsqueeze(1).to_broadcast([128, nh, 2, half])
            sinb = sin.unsqueeze(1).to_broadcast([128, nh, half])
            K.dve(lambda e: e.tensor_tensor(out=tAv, in0=pv, in1=cosb, op=ALU.mult), reads=[pb[bank], rb], writes=[tA])
            K.dve(lambda e: e.tensor_tensor(out=tBv[:, :, 0, :], in0=pv[:, :, 1, :], in1=sinb, op=ALU.mult),
                  reads=[pb[bank], rb], writes=[tB])
            K.dve(lambda e: e.tensor_tensor(out=tBv[:, :, 1, :], in0=pv[:, :, 0, :], in1=sinb, op=ALU.mult),
                  reads=[pb[bank], rb], writes=[tB])
            return tAv, tBv

        if STOP <= 0:
            K.barrier()
            return nc
        with contextlib.ExitStack() as es1:
            load_g1(es1, "g1bc_a")
            w1 = sb(es1, "w1", [128, 8, 3072], BF16, True)
            wro = sb(es1, "wro", [128, 8, D], BF16, True)
            grbc = sb(es1, "grbc", [128, D], F32, True)
            load_w("pool", w1, w_in_d, 0, 0, 3072, 8)
            load_w("pool", wro, wro_d, 0, 0, D, 8)
            b2a = load_bias2(es1, "b2a", 0, 3072)
            K.dma("sp", [(grbc[:], gr_d.partition_broadcast(128))], grbc.sem, writes=[grbc])
            K.pool(lambda e: e.tensor_scalar(out=grbc[:], in0=grbc[:], scalar1=0.5, scalar2=None, op0=ALU.mult),
                   reads=[grbc], writes=[grbc])
            xb_r = ring(es1, "xb", 3, [128, D], F32, True)
            rot_r = ring(es1, "rotb", 2, [128, 192], F32, True)
            hb_r = ring(es1, "hb", 2, [128, D], BF16)
            hT_r = ring(es1, "hT", 2, [128, 8, 128], BF16)
            tA = sb(es1, "tA", [128, 512], F32)
            tB = sb(es1, "tB", [128, 512], F32)
            qk_r = ring(es1, "qk", 2, [128, 2, 512], BF16)
            kd_r = ring(es1, "kd", 2, [128, 512], BF16)
            v_r = ring(es1, "vr", 2, [128, D], BF16)
            th_t = sb(es1, "th", [128, D], F32)
            gs_r = ring(es1, "gs", 2, [128, D], F32)
            S32 = sb(es1, "S32", [128, D], F32)
            Sbf = sb(es1, "Sbf", [128, D], BF16)
            qT = sb(es1, "qT", [128, 4, 128], BF16)
            qdT = sb(es1, "qdT", [128, 4, 128], BF16)
            kT = sb(es1, "kT", [128, 4, 128], BF16)
            PT = sb(es1, "PT", [128, 512], BF16)
            bnst = sb(es1, "bnst", [128, 4, 6], F32)
            bnmv = sb(es1, "bnmv", [128, 4, 2], F32)
            bnst_b = [Buf() for _ in range(4)]
            bnmv_b = [Buf() for _ in range(4)]
            st_sum = sb(es1, "st_sum", [128, 4], F32)
            st_ssq = sb(es1, "st_ssq", [128, 4], F32)
            st_mean = sb(es1, "st_mean", [128, 4], F32)
            st_var = sb(es1, "st_var", [128, 4], F32)
            st_rs = sb(es1, "st_rs", [128, 4], F32)
            gsp = sb(es1, "gsp", [128, D], F32)
            gated = sb(es1, "gated", [128, D], BF16)
            gatedT = sb(es1, "gatedT", [128, 8, 128], BF16)
            ba_r = ring(es1, "ba", 2, [128, D], F32, True)

            K.dve(lambda e: e.memset(S32[:], 0.0), writes=[S32])

            def s1_ret(i, n_stat, xsrc, rotidx, full):
                xb = xb_r[i % 3]
                rb = rot_r[i % 2]
                hb = hb_r[i % 2]
                hT = hT_r[i % 2]
                qk = qk_r[i % 2]
                kd = kd_r[i % 2]
                vr = v_r[i % 2]
                gs = gs_r[i % 2]
                K.dma("sp", [(rb[:], rot_d[rotidx])], rb.sem, writes=[rb])
                front(es1, n_stat, xsrc, xb, hb, hT, 0, True)
                yield
                cos = rb[:, 0:64]
                sin = rb[:, 64:128]
                blocks = ([0] if full else []) + [1]
                for j, cb_ in enumerate(blocks):
                    bank = 2 + (j % 2)
                    proj_block(hT, w1, b2a, cb_ * 512, 512, bank)
                    tAv, tBv = rotary(bank, 4, 64, cos, sin, tA, tB, rb)
                    ov = qk[:, cb_, :].rearrange("p (h t d) -> p h t d", h=4, t=2)
                    K.pool(lambda e: e.tensor_tensor(out=ov[:, :, 0, :], in0=tAv[:, :, 0, :], in1=tBv[:, :, 0, :], op=ALU.subtract),
                           reads=[tA, tB], writes=[qk])
                    K.pool(lambda e: e.tensor_tensor(out=ov[:, :, 1, :], in0=tAv[:, :, 1, :], in1=tBv[:, :, 1, :], op=ALU.add),
                           reads=[tA, tB], writes=[qk])
                    yield
                kdb = KDc.unsqueeze(2).to_broadcast([128, 4, 128])
                K.pool(lambda e: e.tensor_tensor(out=kd[:].rearrange("p (h d) -> p h d", h=4),
                                                 in0=qk[:, 1, :].rearrange("p (h d) -> p h d", h=4), in1=kdb, op=ALU.mult),
                       reads=[qk, cft], writes=[kd])
                for j in range(2):
                    bank = 2 + (j % 2)
                    proj_block(hT, w1, b2a, 1024 + j * 512, 512, bank)
                    K.act(lambda e: e.activation(out=vr[:, j * 512:(j + 1) * 512], in_=PS[:, bank, :], func=AF.Copy),
                          reads=[pb[bank]], writes=[vr])
                    yield
                if full:
                    for j in range(2):
                        bank = 2 + (j % 2)
                        proj_block(hT, w1, b2a, 2048 + j * 512, 512, bank)
                        K.act(lambda e: e.activation(out=th_t[:, j * 512:(j + 1) * 512], in_=PS[:, bank, :], func=AF.Tanh, scale=0.5),
                              reads=[pb[bank]], writes=[th_t])
                        K.dve(lambda e: e.scalar_tensor_tensor(out=gs[:, j * 512:(j + 1) * 512], in0=th_t[:, j * 512:(j + 1) * 512],
                                                               scalar=1.0, in1=PS[:, bank, :], op0=ALU.add, op1=ALU.mult),
                              reads=[th_t, pb[bank]], writes=[gs])
                        yield
                    K.pool(lambda e: e.tensor_tensor(out=gs[:], in0=gs[:], in1=grbc[:], op=ALU.mult), reads=[gs, grbc], writes=[gs])

            def state_update(i):
                kd = kd_r[i % 2]
                vr = v_r[i % 2]

                def mm(e):
                    for h in range(4):
                        bank = 7 if h < 2 else 4
                        ins = e.matmul(PS[:, bank, (h % 2) * 256:(h % 2 + 1) * 256], lhsT=kd[:, h * 128:(h + 1) * 128],
                                       rhs=vr[:, h * 256:(h + 1) * 256], start=True, stop=True)
                    return ins
                K.pe(mm, reads=[kd, vr], writes=[pb[7], pb[4]])
                for h in range(4):
                    bank = 7 if h < 2 else 4
                    K.dve(lambda e: e.scalar_tensor_tensor(out=S32[:, h * 256:(h + 1) * 256], in0=S32[:, h * 256:(h + 1) * 256],
                                                           scalar=CDc[:, h:h + 1],
                                                           in1=PS[:, bank, (h % 2) * 256:(h % 2 + 1) * 256],
                                                           op0=ALU.mult, op1=ALU.add),
                          reads=[S32, cft, pb[bank]], writes=[S32])

            gprev = None
            for n in range(NCH + 1):
                g = None
                if n < NCH:
                    g = s1_ret(n, n, xp_d[n * 128:(n + 1) * 128, :], n, False)
                    next(g)
                if gprev is not None:
                    _drain(gprev)
                    state_update(n - 1)
                gprev = g
            K.dve(lambda e: e.tensor_scalar(out=S32[:], in0=S32[:], scalar1=smask, scalar2=None, op0=ALU.mult),
                  reads=[S32, cft], writes=[S32])
            K.pool(lambda e: e.tensor_copy(out=Sbf[:], in_=S32[:]), reads=[S32], writes=[Sbf])

            def s2_ret(n):
                i = n
                qk = qk_r[i % 2]
                vr = v_r[i % 2]
                gs = gs_r[i % 2]
                ba = ba_r[i % 2]

                def tr(e):
                    for j in range(8):
                        ins = e.transpose(psT(1)[:, j, :], qk[:, j // 4, (j % 4) * 128:(j % 4 + 1) * 128], ident)
                    return ins
                K.pe(tr, reads=[qk, cbt], writes=[pb[1]])
                K.act(lambda e: e.activation(out=qT[:], in_=psT(1)[:, 0:4, :], func=AF.Copy), reads=[pb[1]], writes=[qT])
                K.dve(lambda e: e.tensor_tensor(out=qdT[:], in0=psT(1)[:, 0:4, :], in1=QD.rearrange("p (h c) -> p h c", h=4), op=ALU.mult),
                      reads=[pb[1], cft], writes=[qdT])
                K.act(lambda e: e.activation(out=kT[:], in_=psT(1)[:, 4:8, :], func=AF.Copy), reads=[pb[1]], writes=[kT])
                yield

                def sc(e):
                    for h in range(4):
                        ins = e.matmul(PS[:, 4, h * 128:(h + 1) * 128], lhsT=kT[:, h, :], rhs=qT[:, h, :], start=True, stop=True)
                    return ins
                K.pe(sc, reads=[kT, qT], writes=[pb[4]])
                K.dve(lambda e: e.tensor_tensor(out=PT[:], in0=PS[:, 4, :], in1=DT, op=ALU.mult), reads=[pb[4], cft], writes=[PT])
                yield

                def om(e):
                    for h in range(4):
                        o = PS[:, 5 + h // 2, (h % 2) * 256:(h % 2 + 1) * 256]
                        e.matmul(o, lhsT=PT[:, h * 128:(h + 1) * 128], rhs=vr[:, h * 256:(h + 1) * 256], start=True, stop=False)
                        ins = e.matmul(o, lhsT=qdT[:, h, :], rhs=Sbf[:, h * 256:(h + 1) * 256], start=False, stop=True)
                    return ins
                K.pe(om, reads=[PT, vr, qdT, Sbf], writes=[pb[5], pb[6]])
                yield
                state_update(n)
                K.act(lambda e: e.activation(out=Sbf[:], in_=S32[:], func=AF.Copy), reads=[S32], writes=[Sbf])
                yield
                O = PS[:, 5:7, :].rearrange("p b (h e) -> p (b h) e", h=2)
                for h in range(4):
                    K.dve(lambda e: e.bn_stats(out=bnst[:, h, :], in_=O[:, h, :]), reads=[pb[5], pb[6]], writes=[bnst_b[h]])
                for h in range(4):
                    K.dve(lambda e: e.bn_aggr(out=bnmv[:, h, :], in_=bnst[:, h, :]), reads=[bnst_b[h]], writes=[bnmv_b[h]])
                K.act(lambda e: e.activation(out=st_rs[:], in_=bnmv[:, :, 1], func=AF.Sqrt, bias=epst[:, 0:1], scale=1.0),
                      reads=bnmv_b + [epst], writes=[st_rs])
                K.dve(lambda e: e.reciprocal(out=st_rs[:], in_=st_rs[:]), reads=[st_rs], writes=[st_rs])
                yield
                K.dve(lambda e: e.tensor_tensor(out=gsp[:].rearrange("p (h e) -> p h e", h=4), in0=gs[:].rearrange("p (h e) -> p h e", h=4),
                                                in1=st_rs[:].unsqueeze(2).to_broadcast([128, 4, 256]), op=ALU.mult),
                      reads=[gs, st_rs], writes=[gsp])
                for h in range(4):
                    K.dve(lambda e: e.scalar_tensor_tensor(out=gated[:, h * 256:(h + 1) * 256], in0=O[:, h, :], scalar=bnmv[:, h, 0:1],
                                                           in1=gsp[:, h * 256:(h + 1) * 256], op0=ALU.subtract, op1=ALU.mult),
                          reads=[pb[5], pb[6], bnmv_b[h], gsp], writes=[gated])
                yield

                def tr2(e):
                    for k in range(8):
                        ins = e.transpose(psT(1)[:, k, :], gated[:, k * 128:(k + 1) * 128], ident)
                    return ins
                K.pe(tr2, reads=[gated, cbt], writes=[pb[1]])
                K.act(lambda e: e.activation(out=gatedT[:], in_=psT(1), func=AF.Copy), reads=[pb[1]], writes=[gatedT])
                yield

                def bam(e):
                    for half in range(2):
                        for k in range(8):
                            ins = e.matmul(PS[:, 5 + half, :], lhsT=gatedT[:, k, :], rhs=wro[:, k, half * 512:(half + 1) * 512],
                                           start=(k == 0), stop=(k == 7))
                    return ins
                K.pe(bam, reads=[gatedT, wro], writes=[pb[5], pb[6]])
                K.act(lambda e: e.activation(out=ba[:], in_=PS[:, 5:7, :].rearrange("p b c -> p (b c)"), func=AF.Copy),
                      reads=[pb[5], pb[6]], writes=[ba])
                K.dma("sp", [(bas_d[n * 128:(n + 1) * 128, :], ba[:])], ba.sem, reads=[ba], writes=[bas_b[n]])

            _drain(s1_ret(0, NCH, x_d[0:128, :], NCH, True))
            for n in range(NCH):
                nx = n + 1
                g1 = s1_ret(nx, NCH + nx, x_d[nx * 128:(nx + 1) * 128, :], NCH + nx, True) if nx < NCH else None
                _interleave(g1, s2_ret(n))
            K.barrier()
        if STOP <= 2:
            return nc

        with contextlib.ExitStack() as es2:
            load_g1(es2, "g1bc_b")
            w2 = sb(es2, "w2", [128, 8, 3328], BF16, True)
            wao = sb(es2, "wao", [128, 8, D], BF16, True)
            wo = sb(es2, "wo", [128, 8, D], BF16, True)
            load_w("pool", w2, w_in_d, 0, 3072, 3328, 8)
            load_w("pool", wao, wao_d, 0, 0, D, 8)
            load_w("pool", wo, wo_d, 0, 0, D, 8)
            b2b = load_bias2(es2, "b2b", 3072, 3328)
            esk = sb(es2, "esk", [128, 16], F32, True)
            K.dma("sp", [(esk[:], sinks_d.partition_broadcast(128))], esk.sem, writes=[esk])
            K.act(lambda e: e.activation(out=esk[:], in_=esk[:], func=AF.Exp), reads=[esk], writes=[esk])
            xb_r = ring(es2, "xb2", 3, [128, D], F32, True)
            rot_r = ring(es2, "rotb2", 2, [128, 192], F32, True)
            hb_r = ring(es2, "hb2", 2, [128, D], BF16)
            hT_r = ring(es2, "hT2", 2, [128, 8, 128], BF16)
            tA = sb(es2, "tA2", [128, 512], F32)
            tB = sb(es2, "tB2", [128, 512], F32)
            qr_r = ring(es2, "qr", 2, [128, D], BF16)
            kdup_r = ring(es2, "kdup", 3, [128, 4, 64], BF16)
            vaug_r = ring(es2, "vaug", 3, [128, 2, 65], BF16)
            tha_r = ring(es2, "tha", 2, [128, D], F32)
            thb_r = ring(es2, "thb", 2, [128, D], F32)
            qT2 = sb(es2, "qT2", [128, 8, 128], BF16)
            kT_r = ring(es2, "kT2", 3, [128, 2, 128], BF16)
            Ecur = ring(es2, "Ecur", 2, [128, D], BF16)
            Eprv = ring(es2, "Eprv", 2, [128, D], BF16)
            den = sb(es2, "den", [128, 16], F32)
            ay = sb(es2, "ay", [128, D], BF16)
            ayT = sb(es2, "ayT", [128, 8, 128], BF16)
            bab_r = ring(es2, "bab", 2, [128, D], F32, True)
            m1 = sb(es2, "m1", [128, D], F32)
            m2 = sb(es2, "m2", [128, D], F32)
            mg = sb(es2, "mg", [128, D], BF16)
            mgT = sb(es2, "mgT", [128, 8, 128], BF16)
            x1_r = ring(es2, "x1b", 2, [128, D], F32, True)
            for v in vaug_r:
                K.dve(lambda e: e.memset(v[:], 1.0), writes=[v])

            def s1_att(i, n_stat, xsrc, rotidx, full):
                xb = xb_r[i % 3]
                rb = rot_r[i % 2]
                hb = hb_r[i % 2]
                hT = hT_r[i % 2]
                qr = qr_r[i % 2]
                kdup = kdup_r[i % 3]
                vaug = vaug_r[i % 3]
                tha = tha_r[i % 2]
                thb = thb_r[i % 2]
                K.dma("sp", [(rb[:], rot_d[rotidx])], rb.sem, writes=[rb])
                front(es2, n_stat, xsrc, xb, hb, hT, 0, False)
                yield
                cos = rb[:, 128:160]
                sin = rb[:, 160:192]
                bi = 0
                if full:
                    for j in range(2):
                        bank = 2 + (bi % 2)
                        bi += 1
                        proj_block(hT, w2, b2b, j * 512, 512, bank)
                        tAv, tBv = rotary(bank, 8, 32, cos, sin, tA, tB, rb)
                        ov = qr[:, j * 512:(j + 1) * 512].rearrange("p (h t d) -> p h t d", h=8, t=2)
                        K.pool(lambda e: e.tensor_tensor(out=ov[:, :, 0, :], in0=tAv[:, :, 0, :], in1=tBv[:, :, 0, :], op=ALU.subtract),
                               reads=[tA, tB], writes=[qr])
                        K.pool(lambda e: e.tensor_tensor(out=ov[:, :, 1, :], in0=tAv[:, :, 1, :], in1=tBv[:, :, 1, :], op=ALU.add),
                               reads=[tA, tB], writes=[qr])
                        yield
                bank = 2 + (bi % 2)
                bi += 1
                proj_block(hT, w2, b2b, 1024, 256, bank)
                tAv, tBv = rotary(bank, 2, 32, cos, sin, tA, tB, rb)
                kv = kdup[:].rearrange("p (g r) (t d) -> p g r t d", r=2, t=2)
                for r in range(2):
                    K.pool(lambda e: e.tensor_tensor(out=kv[:, :, r, 0, :], in0=tAv[:, :, 0, :], in1=tBv[:, :, 0, :], op=ALU.subtract),
                           reads=[tA, tB], writes=[kdup])
                    K.pool(lambda e: e.tensor_tensor(out=kv[:, :, r, 1, :], in0=tAv[:, :, 1, :], in1=tBv[:, :, 1, :], op=ALU.add),
                           reads=[tA, tB], writes=[kdup])
                K.act(lambda e: e.activation(out=vaug[:, :, 0:64], in_=PS[:, bank, 128:256].rearrange("p (g d) -> p g d", g=2), func=AF.Copy),
                      reads=[pb[bank]], writes=[vaug])
                yield
                if full:
                    for (tht, c0) in ((tha, 1280), (thb, 2304)):
                        for j in range(2):
                            bank = 2 + (bi % 2)
                            bi += 1
                            proj_block(hT, w2, b2b, c0 + j * 512, 512, bank)
                            K.act(lambda e: e.activation(out=tht[:, j * 512:(j + 1) * 512], in_=PS[:, bank, :], func=AF.Tanh, scale=0.5),
                                  reads=[pb[bank]], writes=[tht])
                            yield

            def k_transpose(i):
                kdup = kdup_r[i % 3]
                kT = kT_r[i % 3]

                def tr(e):
                    for j in range(2):
                        ins = e.transpose(psT(0)[:, j, :], kdup[:, 2 * j:2 * j + 2, :].rearrange("p a d -> p (a d)"), ident)
                    return ins
                K.pe(tr, reads=[kdup, cbt], writes=[pb[0]])
                K.act(lambda e: e.activation(out=kT[:], in_=psT(0)[:, 0:2, :], func=AF.Copy), reads=[pb[0]], writes=[kT])

            def s2_att(n):
                i = n + 1
                xb = xb_r[i % 3]
                qr = qr_r[i % 2]
                vcur = vaug_r[i % 3]
                vprv = vaug_r[(i - 1) % 3]
                kTc = kT_r[i % 3]
                kTp = kT_r[(i - 1) % 3]
                tha = tha_r[i % 2]
                thb = thb_r[i % 2]
                bab = bab_r[n % 2]
                x1b = x1_r[n % 2]
                K.dma("sp", [(bab[:], bas_d[n * 128:(n + 1) * 128, :])], bab.sem, reads=[bas_b[n]], writes=[bab])

                def tr(e):
                    for j in range(8):
                        ins = e.transpose(psT(1)[:, j, :], qr[:, j * 128:(j + 1) * 128], ident)
                    return ins
                K.pe(tr, reads=[qr, cbt], writes=[pb[1]])
                K.act(lambda e: e.activation(out=qT2[:], in_=psT(1), func=AF.Copy), reads=[pb[1]], writes=[qT2])
                k_transpose(i)
                yield
                mp = mprev0 if n == 0 else mprev
                for g in range(2):
                    for (kTx, E, b0, mk) in ((kTc, Ecur[g], 4, mcur), (kTp, Eprv[g], 6, mp)):
                        def sc(e):
                            for p in range(2):
                                ins = e.matmul(PS[:, b0 + p, :], lhsT=kTx[p * 64:(p + 1) * 64, g, :],
                                               rhs=qT2[p * 64:(p + 1) * 64, 4 * g:4 * g + 4, :], start=True, stop=True)
                            return ins
                        K.pe(sc, reads=[kTx, qT2], writes=[pb[b0], pb[b0 + 1]])
                        K.act(lambda e: e.activation(out=E[:], in_=PS[:, b0:b0 + 2, :].rearrange("p b c -> p (b c)"), func=AF.Exp, scale=0.125),
                              reads=[pb[b0], pb[b0 + 1]], writes=[E])
                        (K.pool if b0 == 4 else K.dve)(
                            lambda e: e.tensor_tensor(out=E[:].rearrange("p (a q) -> p a q", a=8), in0=E[:].rearrange("p (a q) -> p a q", a=8),
                                                      in1=mk.unsqueeze(1).to_broadcast([128, 8, 128]), op=ALU.mult),
                            reads=[E, cbt], writes=[E])
                        yield

                def hslot(h):
                    return PS[:, 4 + h // 7, (h % 7) * 65:(h % 7) * 65 + 65]

                def pv(e):
                    for h in range(16):
                        g, p, b = h // 8, h % 2, (h % 8) // 2
                        c0 = p * 512 + b * 128
                        e.matmul(hslot(h), lhsT=Ecur[g][:, c0:c0 + 128], rhs=vcur[:, g, :], start=True, stop=False)
                        ins = e.matmul(hslot(h), lhsT=Eprv[g][:, c0:c0 + 128], rhs=vprv[:, g, :], start=False, stop=True)
                    return ins
                K.pe(pv, reads=[Ecur[0], Ecur[1], Eprv[0], Eprv[1], vcur, vprv], writes=[pb[4], pb[5], pb[6]])
                for (bk, h0, nh) in ((4, 0, 7), (5, 7, 7), (6, 14, 2)):
                    ov = PS[:, bk, 0:nh * 65].rearrange("p (h c) -> p h c", c=65)
                    K.dve(lambda e: e.tensor_tensor(out=den[:, h0:h0 + nh], in0=ov[:, :, 64], in1=esk[:, h0:h0 + nh], op=ALU.add),
                          reads=[pb[bk], esk], writes=[den])
                    K.dve(lambda e: e.reciprocal(out=den[:, h0:h0 + nh], in_=den[:, h0:h0 + nh]), reads=[den], writes=[den])
                    K.dve(lambda e: e.tensor_tensor(out=ay[:, h0 * 64:(h0 + nh) * 64].rearrange("p (h d) -> p h d", d=64), in0=ov[:, :, 0:64],
                                                    in1=den[:, h0:h0 + nh].unsqueeze(2).to_broadcast([128, nh, 64]), op=ALU.mult),
                          reads=[pb[bk], den], writes=[ay])
                yield

                def tr2(e):
                    for k in range(8):
                        ins = e.transpose(psT(1)[:, k, :], ay[:, k * 128:(k + 1) * 128], ident)
                    return ins
                K.pe(tr2, reads=[ay, cbt], writes=[pb[1]])
                K.act(lambda e: e.activation(out=ayT[:], in_=psT(1), func=AF.Copy), reads=[pb[1]], writes=[ayT])
                yield

                def bbm(e):
                    for half in range(2):
                        for k in range(8):
                            ins = e.matmul(PS[:, 4 + half, :], lhsT=ayT[:, k, :], rhs=wao[:, k, half * 512:(half + 1) * 512],
                                           start=(k == 0), stop=(k == 7))
                    return ins
                K.pe(bbm, reads=[ayT, wao], writes=[pb[4], pb[5]])
                K.dve(lambda e: e.scalar_tensor_tensor(out=m1[:], in0=tha[:], scalar=1.0, in1=bab[:], op0=ALU.add, op1=ALU.mult),
                      reads=[tha, bab], writes=[m1])
                K.dve(lambda e: e.scalar_tensor_tensor(out=m2[:], in0=thb[:], scalar=1.0, in1=PS[:, 4:6, :].rearrange("p b c -> p (b c)"),
                                                       op0=ALU.add, op1=ALU.mult),
                      reads=[thb, pb[4], pb[5]], writes=[m2])
                K.dve(lambda e: e.tensor_tensor(out=mg[:], in0=m1[:], in1=m2[:], op=ALU.add), reads=[m1, m2], writes=[mg])
                yield

                def tr3(e):
                    for k in range(8):
                        ins = e.transpose(psT(1)[:, k, :], mg[:, k * 128:(k + 1) * 128], ident)
                    return ins
                K.pe(tr3, reads=[mg, cbt], writes=[pb[1]])
                K.act(lambda e: e.activation(out=mgT[:], in_=psT(1), func=AF.Copy), reads=[pb[1]], writes=[mgT])
                yield

                def xom(e):
                    for half in range(2):
                        for k in range(8):
                            ins = e.matmul(PS[:, 6 + half, :], lhsT=mgT[:, k, :], rhs=wo[:, k, half * 512:(half + 1) * 512],
                                           start=(k == 0), stop=(k == 7))
                    return ins
                K.pe(xom, reads=[mgT, wo], writes=[pb[6], pb[7]])
                K.dve(lambda e: e.scalar_tensor_tensor(out=x1b[:], in0=PS[:, 6:8, :].rearrange("p b c -> p (b c)"), scalar=0.5, in1=xb[:],
                                                       op0=ALU.mult, op1=ALU.add),
                      reads=[pb[6], pb[7], xb], writes=[x1b])
                junk = nextjunk()
                K.act(lambda e: e.activation(out=junk[:], in_=x1b[:], func=AF.Square, scale=1.0 / 32.0, accum_out=ssq2[:, n:n + 1]),
                      reads=[x1b], writes=[junk, ssq2_b[n]], strict=True)
                K.dma("sp", [(x1s_d[n * 128:(n + 1) * 128, :], x1b[:])], x1b.sem, reads=[x1b], writes=[x1s_b[n]])

            _drain(s1_att(0, NCH - 1, xp_d[(NCH - 1) * 128:NCH * 128, :], NCH - 1, False))
            k_transpose(0)
            _drain(s1_att(1, NCH, x_d[0:128, :], NCH, True))
            for n in range(NCH):
                nx = n + 1
                g1 = s1_att(nx + 1, NCH + nx, x_d[nx * 128:(nx + 1) * 128, :], NCH + nx, True) if nx < NCH else None
                _interleave(g1, s2_att(n))
            K.barrier()
        if STOP <= 3:
            return nc

        with contextlib.ExitStack() as es3:
            wg = sb(es3, "wg", [128, 8, DFF], BF16, True)
            wu = sb(es3, "wu", [128, 8, DFF], BF16, True)
            wd = sb(es3, "wd", [128, NFF, D], BF16, True)
            load_w("pool", wg, wg_d, 0, 0, DFF, 8)
            load_w("pool", wu, wu_d, 0, 0, DFF, 8)
            load_w("pool", wd, wd_d, 0, 0, D, NFF)
            g2bc = sb(es3, "g2bc", [128, D], F32, True)
            gfbc = sb(es3, "gfbc", [128, D], F32, True)
            K.dma("sp", [(g2bc[:], g2_d.partition_broadcast(128))], g2bc.sem, writes=[g2bc])
            K.dma("sp", [(gfbc[:], gf_d.partition_broadcast(128))], gfbc.sem, writes=[gfbc])
            xa_r = ring(es3, "xa", 1, [128, D], F32, True)
            h2_r = ring(es3, "h2b", 1, [128, D], BF16)
            h2T_r = ring(es3, "h2T", 2, [128, 8, 512], BF16)
            h2T_bs = [[Buf() for _ in range(4)] for _ in range(2)]
            sg_r = ring(es3, "sg", 2, [128, 512], F32)
            actT = sb(es3, "actT", [128, NFF, 512], BF16)
            actT_b = [Buf() for _ in range(NFF)]
            xr_r = ring(es3, "xr", 2, [128, D], F32, True)
            ssq3 = sb(es3, "ssq3", [128, 4], F32)
            rs3 = sb(es3, "rs3", [128, 4], F32)
            s3_b = [Buf() for _ in range(4)]
            ssq2_all = Buf()
            K.act(lambda e: e.activation(out=rstd2[:], in_=ssq2[:], func=AF.Sqrt, bias=epst[:, 0:1], scale=1.0),
                  reads=ssq2_b + [epst], writes=[ssq2_all])
            K.dve(lambda e: e.reciprocal(out=rstd2[:], in_=rstd2[:]), reads=[ssq2_all], writes=[ssq2_all])

            def p3_front(t):
                h2T = h2T_r[t % 2]
                h2T_b = h2T_bs[t % 2]
                for j in range(4):
                    n = 4 * t + j
                    xa = xa_r[0]
                    h2b = h2_r[0]
                    K.dma("sp", [(xa[:], x1s_d[n * 128:(n + 1) * 128, :])], xa.sem, reads=[x1s_b[n]], writes=[xa])
                    K.dve(lambda e: e.scalar_tensor_tensor(out=h2b[:], in0=xa[:], scalar=rstd2[:, n:n + 1], in1=g2bc[:],
                                                           op0=ALU.mult, op1=ALU.mult),
                          reads=[xa, ssq2_all, g2bc], writes=[h2b])
                    tb = j % 2

                    def tr(e):
                        for k in range(8):
                            ins = e.transpose(psT(tb)[:, k, :], h2b[:, k * 128:(k + 1) * 128], ident)
                        return ins
                    K.pe(tr, reads=[h2b, cbt], writes=[pb[tb]])
                    K.act(lambda e: e.activation(out=h2T[:, :, j * 128:(j + 1) * 128], in_=psT(tb), func=AF.Copy),
                          reads=[pb[tb]], writes=[h2T_b[j]])

            def p3_gu(t):
                h2T = h2T_r[t % 2]
                h2T_b = h2T_bs[t % 2]
                for f in range(NFF):
                    gb_, ub_ = (2, 3) if f % 2 == 0 else (4, 5)
                    sg = sg_r[f % 2]

                    def gm(e):
                        for k in range(8):
                            ins = e.matmul(PS[:, gb_, :], lhsT=wg[:, k, f * 128:(f + 1) * 128], rhs=h2T[:, k, :], start=(k == 0), stop=(k == 7))
                        return ins
                    K.pe(gm, reads=[wg] + h2T_b, writes=[pb[gb_]])

                    def um(e):
                        for k in range(8):
                            ins = e.matmul(PS[:, ub_, :], lhsT=wu[:, k, f * 128:(f + 1) * 128], rhs=h2T[:, k, :], start=(k == 0), stop=(k == 7))
                        return ins
                    K.pe(um, reads=[wu] + h2T_b, writes=[pb[ub_]])
                    K.act(lambda e: e.activation(out=sg[:], in_=PS[:, gb_, :], func=AF.Silu), reads=[pb[gb_]], writes=[sg])
                    K.dve(lambda e: e.tensor_tensor(out=actT[:, f, :], in0=sg[:], in1=PS[:, ub_, :], op=ALU.mult),
                          reads=[sg, pb[ub_]], writes=[actT_b[f]])

            def p3_down(t):
                for j in range(4):
                    n = 4 * t + j
                    xr = xr_r[n % 2]
                    b0 = 6 if j % 2 == 0 else 4
                    K.dma("sp", [(xr[:], x1s_d[n * 128:(n + 1) * 128, :])], xr.sem, reads=[x1s_b[n]], writes=[xr])

                    def dm(e):
                        for half in range(2):
                            for f in range(NFF):
                                ins = e.matmul(PS[:, b0 + half, :], lhsT=actT[:, f, j * 128:(j + 1) * 128],
                                               rhs=wd[:, f, half * 512:(half + 1) * 512], start=(f == 0), stop=(f == NFF - 1))
                        return ins
                    K.pe(dm, reads=[wd] + actT_b, writes=[pb[b0], pb[b0 + 1]])
                    K.dve(lambda e: e.tensor_tensor(out=xr[:], in0=xr[:], in1=PS[:, b0:b0 + 2, :].rearrange("p b c -> p (b c)"), op=ALU.add),
                          reads=[xr, pb[b0], pb[b0 + 1]], writes=[xr])
                    junk = nextjunk()
                    K.act(lambda e: e.activation(out=junk[:], in_=xr[:], func=AF.Square, scale=1.0 / 32.0, accum_out=ssq3[:, j:j + 1]),
                          reads=[xr], writes=[junk, s3_b[j]], strict=True)
                    K.act(lambda e: e.activation(out=rs3[:, j:j + 1], in_=ssq3[:, j:j + 1], func=AF.Sqrt, bias=epst[:, 0:1], scale=1.0),
                          reads=[s3_b[j], epst], writes=[s3_b[j]])
                    K.dve(lambda e: e.reciprocal(out=rs3[:, j:j + 1], in_=rs3[:, j:j + 1]), reads=[s3_b[j]], writes=[s3_b[j]])
                    K.dve(lambda e: e.scalar_tensor_tensor(out=xr[:], in0=xr[:], scalar=rs3[:, j:j + 1], in1=gfbc[:], op0=ALU.mult, op1=ALU.mult),
                          reads=[xr, s3_b[j], gfbc], writes=[xr])
                    K.dma("sp", [(out_d[n * 128:(n + 1) * 128, :], xr[:])], xr.sem, reads=[xr])

            NT = NCH // 4
            p3_front(0)
            for t in range(NT):
                p3_gu(t)
                if t + 1 < NT:
                    p3_front(t + 1)
                p3_down(t)
            K.barrier()
    return nc


def _consts(first_half):
    H = 4
    C = 128
    lg = np.log1p(-np.exp2(-5.0 - np.arange(H, dtype=np.float64)))
    idx = np.arange(C, dtype=np.float64)
    scale = 128.0 ** -0.5
    cf = np.zeros((128, NCF), np.float64)
    rel = idx[None, :] - idx[:, None]
    for h in range(H):
        cf[:, h * 128:(h + 1) * 128] = np.where(rel >= 0, np.exp(lg[h] * np.maximum(rel, 0.0)), 0.0) * scale
        cf[:, 512 + h * 128:512 + (h + 1) * 128] = (np.exp(lg[h] * (idx + 1.0)) * scale)[None, :]
        cf[:, 1024 + h] = np.exp(lg[h] * (C - 1.0 - idx))
        cf[:, 1028 + h] = np.exp(lg[h] * C)
    cf[:, 1032] = 0.0 if first_half else 1.0
    cb = np.zeros((128, NCB), np.float32)
    cb[:, 0:128] = np.eye(128)
    kj = np.arange(128)[:, None]
    qi = np.arange(128)[None, :]
    cb[:, 128:256] = (kj <= qi)
    cb[:, 256:384] = (kj > qi)
    cb[:, 384:512] = 0.0 if first_half else (kj > qi)
    cb[:, 512:640] = 1.0
    return cf.astype(np.float32), cb


def _rot(pos):
    pos = pos.astype(np.float32)
    out = np.zeros(pos.shape + (192,), np.float32)
    inv_r = (10000.0 ** (-np.arange(64, dtype=np.float32) / 64)).astype(np.float32)
    inv_a = (10000.0 ** (-np.arange(32, dtype=np.float32) / 32)).astype(np.float32)
    ang_r = pos[..., None] * inv_r
    ang_a = pos[..., None] * inv_a
    out[..., 0:64] = np.cos(ang_r)
    out[..., 64:128] = np.sin(ang_r)
    out[..., 128:160] = np.cos(ang_a)
    out[..., 160:192] = np.sin(ang_a)
    return out


_NC_CACHE = {}


def kernel(x, ln1_g, w_in, b_in, ret_norm_g, w_ret_out, attn_sinks, w_attn_out, w_out,
           ln2_g, w_ffn_gate, w_ffn_up, w_ffn_down, lnf_g):
    f = lambda a: np.ascontiguousarray(np.asarray(a, dtype=np.float32))
    x = f(x)
    B, S, _ = x.shape
    T = S // 2
    NCH = T // 128
    if NCH not in _NC_CACHE:
        _NC_CACHE[NCH] = build(NCH)
    nc = _NC_CACHE[NCH]
    shared = {
        "w_in": f(w_in)[0], "b_in": f(b_in)[0][None, :], "ln1_g": f(ln1_g)[0][None, :], "ret_norm_g": f(ret_norm_g)[0][None, :],
        "w_ret_out": f(w_ret_out)[0], "attn_sinks": f(attn_sinks)[0][None, :], "w_attn_out": f(w_attn_out)[0], "w_out": f(w_out)[0],
        "ln2_g": f(ln2_g)[0][None, :], "w_ffn_gate": f(w_ffn_gate)[0], "w_ffn_up": f(w_ffn_up)[0], "w_ffn_down": f(w_ffn_down)[0],
        "lnf_g": f(lnf_g)[None, :],
    }
    chunkpos = np.arange(S, dtype=np.int64).reshape(2 * NCH, 128)
    in_maps = []
    for b in range(B):
        for half in range(2):
            cf, cb = _consts(half == 0)
            if half == 0:
                xp = np.zeros((T, D), np.float32)
                pos = np.concatenate([chunkpos[:NCH], chunkpos[:NCH]], 0)
            else:
                xp = x[b, :T]
                pos = chunkpos
            m = dict(shared)
            m.update({"x": np.ascontiguousarray(x[b, half * T:(half + 1) * T]), "xp": np.ascontiguousarray(xp),
                      "rot": _rot(pos), "cf": cf, "cb": cb})
            in_maps.append(m)
    res = run_bass_kernel_spmd(nc, in_maps, core_ids=list(range(2 * B)))
    out = np.empty((B, S, D), np.float32)
    for b in range(B):
        for half in range(2):
            out[b, half * T:(half + 1) * T] = res.results[2 * b + half]["out"]
    return out
```
